# Optimizing a Trainium2 kernel written in Bass

```python
import math
import jax, jax.numpy as jnp
from jax import lax
import numpy as np

D_MODEL = 1024
BATCH = 8
SEQ = 2048
DEPTH = 1

CHUNK = 64
D_RNN = 1280
H_RNN = 16
RNN_BLOCK = D_RNN // H_RNN
CONV_A = 4
LRU_C = 8.0
H_ATT = 16
HEAD_DIM = 64
D_ATT = H_ATT * HEAD_DIM
Q_BLOCK = 128
D_FF = 3 * D_MODEL
CONV_F = 3
N_BRANCH = 2
D_IN = 2 * D_RNN + 3 * D_ATT + H_ATT + N_BRANCH * D_MODEL
RMS_EPS = 1e-6

kernel_name = "hybrid_rglru_forgetting_attn_convffn_block"


def rms_norm(x, gain):
    xf = x.astype(jnp.float32)
    y = xf * lax.rsqrt(jnp.mean(xf * xf, axis=-1, keepdims=True) + RMS_EPS)
    return (y * gain.astype(jnp.float32)).astype(x.dtype)


def causal_dwconv(x, w, b):
    k_w, c = w.shape
    y = lax.conv_general_dilated(
        x, w[:, None, :].astype(x.dtype), window_strides=(1,), padding=[(k_w - 1, 0)],
        dimension_numbers=("NWC", "WIO", "NWC"), feature_group_count=c)
    return y + b.astype(x.dtype)


def rg_lru(xc, w_a, b_a, w_x, b_x, lam):
    bsz, s, _ = xc.shape
    xb = xc.reshape(bsz, s, H_RNN, RNN_BLOCK)
    r = jax.nn.sigmoid(jnp.einsum("bshi,hij->bshj", xb, w_a).reshape(bsz, s, D_RNN) + b_a)
    i = jax.nn.sigmoid(jnp.einsum("bshi,hij->bshj", xb, w_x).reshape(bsz, s, D_RNN) + b_x)
    log_a = -LRU_C * r.astype(jnp.float32) * jax.nn.softplus(-lam.astype(jnp.float32))
    a = jnp.exp(log_a)
    b_term = jnp.sqrt(-jnp.expm1(2.0 * log_a)) * (i * xc).astype(jnp.float32)

    def combine(left, right):
        a1, h1 = left
        a2, h2 = right
        return a1 * a2, a2 * h1 + h2

    _, h = lax.associative_scan(combine, (a, b_term), axis=1)
    return h.astype(xc.dtype)


def forgetting_attention(q, k, v, f_logit):
    bsz, s, _ = q.shape
    q = q.reshape(bsz, s, H_ATT, HEAD_DIM).transpose(0, 2, 1, 3)
    k = k.reshape(bsz, s, H_ATT, HEAD_DIM).transpose(0, 2, 1, 3)
    v = v.reshape(bsz, s, H_ATT, HEAD_DIM).transpose(0, 2, 1, 3)
    cum = jnp.cumsum(jax.nn.log_sigmoid(f_logit.astype(jnp.float32)), axis=1).transpose(0, 2, 1)
    scale = 1.0 / math.sqrt(HEAD_DIM)
    outs = []
    for blk in range(s // Q_BLOCK):
        qs, qe = blk * Q_BLOCK, (blk + 1) * Q_BLOCK
        sc = jnp.einsum("bhqd,bhkd->bhqk", q[:, :, qs:qe], k[:, :, :qe]).astype(jnp.float32) * scale
        sc = sc + cum[:, :, qs:qe, None] - cum[:, :, None, :qe]
        allowed = jnp.arange(qe)[None, :] <= jnp.arange(qs, qe)[:, None]
        sc = jnp.where(allowed, sc, -jnp.inf)
        p = jax.nn.softmax(sc, axis=-1).astype(v.dtype)
        outs.append(jnp.einsum("bhqk,bhkd->bhqd", p, v[:, :, :qe]))
    o = jnp.concatenate(outs, axis=2)
    return o.transpose(0, 2, 1, 3).reshape(bsz, s, D_ATT)


def setup_inputs(seed: int = 0) -> dict:
    key = jax.random.key(seed)
    ks = jax.random.split(key, 24)
    f32 = jnp.float32
    nrm = lambda k, shape, fan_in: jax.random.normal(k, shape, f32) * (fan_in ** -0.5)
    gain = lambda k: 1.0 + 0.05 * jax.random.normal(k, (DEPTH, D_MODEL), f32)
    small = lambda k, shape: 0.01 * jax.random.normal(k, shape, f32)
    u = jax.random.uniform(ks[9], (DEPTH, D_RNN), f32, 0.9, 0.999)
    a0 = u ** (1.0 / LRU_C)
    lru_lambda = jnp.log(a0) - jnp.log1p(-a0)
    return {
        "x": jax.random.normal(ks[0], (BATCH, SEQ, D_MODEL), f32),
        "mix_norm_pre": gain(ks[1]),
        "mix_norm_post": gain(ks[2]),
        "w_in": nrm(ks[3], (DEPTH, D_MODEL, D_IN), D_MODEL),
        "conv_a_w": nrm(ks[4], (DEPTH, CONV_A, D_RNN), CONV_A),
        "conv_a_b": small(ks[5], (DEPTH, D_RNN)),
        "w_rg_a": nrm(ks[6], (DEPTH, H_RNN, RNN_BLOCK, RNN_BLOCK), RNN_BLOCK),
        "b_rg_a": small(ks[7], (DEPTH, D_RNN)),
        "w_rg_x": nrm(ks[8], (DEPTH, H_RNN, RNN_BLOCK, RNN_BLOCK), RNN_BLOCK),
        "b_rg_x": small(ks[10], (DEPTH, D_RNN)),
        "lru_lambda": lru_lambda,
        "b_forget": jax.random.uniform(ks[11], (DEPTH, H_ATT), f32, 1.0, 4.0),
        "b_merge": small(ks[12], (DEPTH, N_BRANCH, D_MODEL)),
        "w_proj_a": nrm(ks[13], (DEPTH, D_RNN, D_MODEL), D_RNN),
        "w_proj_b": nrm(ks[14], (DEPTH, D_ATT, D_MODEL), D_ATT),
        "w_out": nrm(ks[15], (DEPTH, D_MODEL, D_MODEL), D_MODEL),
        "ffn_norm_pre": gain(ks[16]),
        "ffn_norm_post": gain(ks[17]),
        "w_up": nrm(ks[18], (DEPTH, D_MODEL, 2 * D_FF), D_MODEL),
        "conv_f_w": nrm(ks[19], (DEPTH, CONV_F, 2 * D_FF), CONV_F),
        "conv_f_b": small(ks[20], (DEPTH, 2 * D_FF)),
        "w_down": nrm(ks[21], (DEPTH, D_FF, D_MODEL), D_FF),
    }


def reference(x, mix_norm_pre, mix_norm_post, w_in, conv_a_w, conv_a_b, w_rg_a, b_rg_a,
              w_rg_x, b_rg_x, lru_lambda, b_forget, b_merge, w_proj_a, w_proj_b, w_out,
              ffn_norm_pre, ffn_norm_post, w_up, conv_f_w, conv_f_b, w_down):
    offs = np.cumsum([D_RNN, D_RNN, D_ATT, D_ATT, D_ATT, H_ATT, D_MODEL]).tolist()
    for layer in range(DEPTH):
        h = rms_norm(x, mix_norm_pre[layer])
        proj = jnp.einsum("bsd,de->bse", h, w_in[layer])
        xa, ga, q, k, v, f_logit, g_a, g_b = jnp.split(proj, offs, axis=-1)
        xc = causal_dwconv(xa, conv_a_w[layer], conv_a_b[layer])
        hr = rg_lru(xc, w_rg_a[layer], b_rg_a[layer], w_rg_x[layer], b_rg_x[layer], lru_lambda[layer])
        y_a = jax.nn.gelu(ga) * hr
        y_b = forgetting_attention(q, k, v, f_logit + b_forget[layer])
        merged = (jax.nn.sigmoid(g_a + b_merge[layer, 0]) * jnp.einsum("bse,ed->bsd", y_a, w_proj_a[layer])
                  + jax.nn.sigmoid(g_b + b_merge[layer, 1]) * jnp.einsum("bse,ed->bsd", y_b, w_proj_b[layer]))
        mix_out = jnp.einsum("bsd,de->bse", merged, w_out[layer])
        x = x + rms_norm(mix_out, mix_norm_post[layer])
        h = rms_norm(x, ffn_norm_pre[layer])
        up = jnp.einsum("bsd,df->bsf", h, w_up[layer])
        up = causal_dwconv(up, conv_f_w[layer], conv_f_b[layer])
        gate, val = jnp.split(up, 2, axis=-1)
        ffn_out = jnp.einsum("bsf,fd->bsd", jax.nn.gelu(gate) * val, w_down[layer])
        x = x + rms_norm(ffn_out, ffn_norm_post[layer])
    return x
```

```python
import numpy as np
import concourse.bass as bass
import concourse.mybir as mybir
from concourse.bass_utils import run_bass_kernel_spmd
from contextlib import ExitStack
import struct

F32 = mybir.dt.float32
BF16 = mybir.dt.bfloat16
AF = mybir.ActivationFunctionType
ALU = mybir.AluOpType

T = 2048
D = 1024
NT = 16
D_RNN = 1280
D_ATT = 1024
D_FF = 3072
D_IN = 7696
O_XA, O_GA, O_Q, O_K, O_V, O_F, O_GMA, O_GMB = 0, 1280, 2560, 3584, 4608, 5632, 5648, 6672
EPS = 1e-6
ONES_BF16X2 = struct.unpack("<f", struct.pack("<I", 0x3F803F80))[0]
KB = 256
ARENA_KB = 192


class Buf:
    def __init__(self, t, name, excl=False, rng=None):
        self.t = t
        self.name = name
        self.w = None
        self.r = []
        self.excl = excl
        self.rng = rng

    def __getitem__(self, idx):
        return self.t[idx]


class Prog:
    ENGS = ["pe", "act", "dve", "pool", "sp"]

    def __init__(self, nc, stack):
        self.nc = nc
        self.stack = stack
        self.sems = {}
        self.res = {}
        self.arena = None
        self.cur = None
        self.e = None
        self.needed = {}
        self.rank = None
        self.reset()
        for k in ["pe", "act", "dve", "pool"]:
            self.newsem(k)

    def reset(self):
        self.cnt = {k: 0 for k in self.sems}
        self.seen = {k: {} for k in self.ENGS}
        self.abufs = []

    def newsem(self, key):
        if key not in self.sems:
            self.sems[key] = self.stack.enter_context(self.nc.semaphore("s_" + key))
        self.cnt.setdefault(key, 0)

    def sbuf(self, name, shape, dt):
        if name not in self.res:
            self.res[name] = self.stack.enter_context(self.nc.sbuf_tensor(name, list(shape), dt))
        return Buf(self.res[name], name)

    def psum(self, name, shape, dt):
        if name not in self.res:
            self.res[name] = self.stack.enter_context(self.nc.psum_tensor(name, list(shape), dt))
        return Buf(self.res[name], name, excl=True)

    def mk(self, offkb, shape, dt, name, nsub=1):
        n = int(np.prod(shape))
        words = (n + 1) // 2 if dt == BF16 else n
        off = int(round(offkb * KB))
        assert off + words <= ARENA_KB * KB, (name, offkb, words)
        ap = self.arena[:, off:off + words]
        if dt == BF16:
            ap = ap.bitcast(BF16)
        if len(shape) == 2:
            ap = ap.rearrange("p (a b) -> p a b", a=shape[0])
        elif len(shape) == 3:
            ap = ap.rearrange("p (a b c) -> p a b c", a=shape[0], b=shape[1])
        elif len(shape) == 4:
            ap = ap.rearrange("p (a b c d) -> p a b c d", a=shape[0], b=shape[1], c=shape[2])
        deps = []
        keep = []
        for ob in self.abufs:
            s, e = ob.rng
            if s < off + words and off < e:
                if ob.w is not None:
                    deps.append(ob.w)
                deps.extend(ob.r)
            keep.append(ob)
        self.abufs = keep
        out = []
        for i in range(nsub):
            b = Buf(ap, f"{name}{i}", rng=(off, off + words))
            b.r = list(deps)
            self.abufs.append(b)
            out.append(b)
        return out[0] if nsub == 1 else out

    def alias(self, buf, name):
        b = Buf(buf.t, name, excl=buf.excl, rng=buf.rng)
        b.w = buf.w
        b.r = list(buf.r)
        self.abufs.append(b)
        return b

    def _deps(self, reads, writes, eng, skip=None, dma=False):
        need = {}

        def add(d, allow_same):
            if d is None:
                return
            k, v = d
            if k == skip:
                return
            if k == eng and not allow_same and not dma:
                return
            if need.get(k, 0) < v:
                need[k] = v

        for b in reads:
            add(b.w, True)
            if b.excl:
                for d in b.r:
                    add(d, False)
        for b in writes:
            add(b.w, False)
            for d in b.r:
                add(d, False)
        waits = []
        for k, v in need.items():
            if self.seen[eng].get(k, 0) < v:
                self.seen[eng][k] = v
                waits.append((k, v))
        return waits

    CE = ("pe", "act", "dve", "pool")

    def _note(self, waits):
        if self.rank is None:
            for k, v in waits:
                if k in self.CE:
                    self.needed.setdefault(k, set()).add(v)

    def _wv(self, k, v):
        return self.rank[k][v] if k in self.CE else v

    def op(self, eng, fn, reads=(), writes=(), sig=True):
        waits = self._deps(reads, writes, eng)
        self._note(waits)
        if sig:
            self.cnt[eng] += 1
            v = self.cnt[eng]
        else:
            v = self.cnt[eng] + 1
        for b in reads:
            b.r.append((eng, v))
        for b in writes:
            b.w = (eng, v)
            b.r = []
        if self.cur == eng:
            e = self.e
            for k, val in waits:
                e.wait_ge(self.sems[k], self._wv(k, val))
            ins = fn(e)
            if sig and v in self.rank[eng]:
                ins.then_inc(self.sems[eng], 1)

    def dma(self, q, out_ap, in_ap, reads=(), writes=(), sem=None, **kw):
        self.newsem(sem)
        waits = self._deps(reads, writes, q, skip=sem, dma=True)
        self._note(waits)
        self.cnt[sem] += 16
        v = self.cnt[sem]
        for b in reads:
            b.r.append((sem, v))
        for b in writes:
            b.w = (sem, v)
            b.r = []
        if self.cur == q:
            e = self.e
            for k, val in waits:
                e.wait_ge(self.sems[k], self._wv(k, val))
            e.dma_start(out=out_ap, in_=in_ap, **kw).then_inc(self.sems[sem], 16)

    def wait_all(self, q, keys):
        if self.cur == q:
            for k in keys:
                if self.cnt.get(k, 0) > 0:
                    self.e.wait_ge(self.sems[k], self.cnt[k])

    def run(self, program):
        def one(eng, e):
            self.cur, self.e = eng, e
            self.reset()
            program()

        one(None, None)
        self.rank = {k: {v: i + 1 for i, v in enumerate(sorted(self.needed.get(k, ())))} for k in self.CE}
        with self.nc.Block() as block:
            @block.tensor
            def _(e):
                one("pe", e)

            @block.scalar
            def _(e):
                one("act", e)

            @block.vector
            def _(e):
                one("dve", e)

            @block.gpsimd
            def _(e):
                one("pool", e)

            @block.sync
            def _(e):
                one("sp", e)


def build(debug=False):
    nc = bass.Bass("TRN2", target_bir_lowering=False)

    def din(name, shape):
        return nc.dram_tensor(name, list(shape), F32, kind="ExternalInput").ap()

    x = din("x", [T, D])
    mix_norm_pre = din("mix_norm_pre", [D])
    mix_norm_post = din("mix_norm_post", [D])
    w_in = din("w_in", [D, D_IN])
    conv_a_w = din("conv_a_w", [4, D_RNN])
    conv_a_b = din("conv_a_b", [D_RNN])
    w_rg_a = din("w_rg_a", [16, 80, 80])
    b_rg_a = din("b_rg_a", [D_RNN])
    w_rg_x = din("w_rg_x", [16, 80, 80])
    b_rg_x = din("b_rg_x", [D_RNN])
    lru_lambda = din("lru_lambda", [D_RNN])
    b_forget = din("b_forget", [16])
    b_merge = din("b_merge", [2, D])
    w_proj_a = din("w_proj_a", [D_RNN, D])
    w_proj_b = din("w_proj_b", [D_ATT, D])
    w_out = din("w_out", [D, D])
    ffn_norm_pre = din("ffn_norm_pre", [D])
    ffn_norm_post = din("ffn_norm_post", [D])
    w_up = din("w_up", [D, 2 * D_FF])
    conv_f_w = din("conv_f_w", [3, 2 * D_FF])
    conv_f_b = din("conv_f_b", [2 * D_FF])
    w_down = din("w_down", [D_FF, D])
    y = nc.dram_tensor("y", [T, D], F32, kind="ExternalOutput").ap()
    dbg = {}
    if debug:
        dbg["hT"] = nc.dram_tensor("dbg_hT", [128, 8, T], BF16, kind="ExternalOutput").ap()
        dbg["y_b"] = nc.dram_tensor("dbg_y_b", [128, 8, T], BF16, kind="ExternalOutput").ap()
        dbg["y_a"] = nc.dram_tensor("dbg_y_a", [128, 10, T], BF16, kind="ExternalOutput").ap()
        dbg["merged"] = nc.dram_tensor("dbg_merged", [128, 8, T], BF16, kind="ExternalOutput").ap()
        dbg["x1"] = nc.dram_tensor("dbg_x1", [128, NT, D], F32, kind="ExternalOutput").ap()
        dbg["Btab"] = nc.dram_tensor("dbg_Btab", [128, 16, 8, 16], F32, kind="ExternalOutput").ap()

    def dap(t, off, pat):
        return bass.AP(t.tensor, off, pat)

    with ExitStack() as st:
        P = Prog(nc, st)
        arena_t = st.enter_context(nc.sbuf_tensor("arena", [128, ARENA_KB * KB], F32))
        P.arena = arena_t

        def program():
            ident = P.sbuf("ident", [128, 128], F32)
            Umat = P.sbuf("Umat", [128, 128], F32)
            Ubc = P.sbuf("Ubc", [128, 128], F32)
            ones = P.sbuf("ones", [128, 128], F32)
            caw = P.sbuf("caw", [128, 10, 4], F32)
            cab = P.sbuf("cab", [128, 10], F32)
            bra = P.sbuf("bra", [128, 10], F32)
            brx = P.sbuf("brx", [128, 10], F32)
            lam = P.sbuf("lam", [128, 10], F32)
            lrc = P.sbuf("lrc", [128, 10], F32)
            lrc2 = P.sbuf("lrc2", [128, 10], F32)
            hbra = P.sbuf("hbra", [128, 10], F32)
            hbrx = P.sbuf("hbrx", [128, 10], F32)
            g1T = P.sbuf("g1T", [128, 8], F32)
            g3T = P.sbuf("g3T", [128, 8], F32)
            bmg = P.sbuf("bmg", [128, 2, 8], F32)
            cfw = P.sbuf("cfw", [128, 48, 3], F32)
            cfb = P.sbuf("cfb", [128, 48], F32)
            bfb = P.sbuf("bfb", [128, 16], F32)
            carry = P.sbuf("carry", [128, 48, 2], F32)
            ss = P.sbuf("ss", [128, 4], F32)
            ps = [P.psum(f"ps{k}", [128, 512], F32) for k in range(8)]

            cd = dict(sem="d_const", allow_slow_non_contiguous=True)
            P.dma("sp", g1T[:], dap(mix_norm_pre, 0, [[1, 128], [128, 8]]), writes=[g1T], sem="d_g1T", allow_slow_non_contiguous=True)
            P.op("pool", lambda e: e.memset(ident[:], 1.0), writes=[ident])
            P.op("pool", lambda e: e.affine_select(out=ident[:], in_=ident[:], pattern=[[1, 128]], compare_op=ALU.is_equal,
                                                   fill=0.0, base=0, channel_multiplier=-1), reads=[ident], writes=[ident])

            def const_dmas_early():
                P.dma("sp", bfb[:], dap(b_forget, 0, [[0, 128], [1, 16]]), writes=[bfb], sem="d_bfb", allow_slow_non_contiguous=True)

            def const_dmas_rg():
                grp = [caw, cab, bra, brx, lam, bmg]
                for kk in range(4):
                    P.dma("act", caw[:, :, kk], dap(conv_a_w, D_RNN * kk, [[1, 128], [128, 10]]), writes=[caw], **cd)
                for tbuf, src in ((cab, conv_a_b), (bra, b_rg_a), (brx, b_rg_x), (lam, lru_lambda)):
                    P.dma("act", tbuf[:], dap(src, 0, [[1, 128], [128, 10]]), writes=[tbuf], **cd)
                for kk in range(2):
                    P.dma("act", bmg[:, kk, :], dap(b_merge, D * kk, [[1, 128], [128, 8]]), writes=[bmg], **cd)
                tot = P.cnt["d_const"]
                for b in grp:
                    b.w = ("d_const", tot)

            def const_dmas_ffn():
                cd2 = dict(sem="d_const2", allow_slow_non_contiguous=True)
                P.dma("act", g3T[:], dap(ffn_norm_pre, 0, [[1, 128], [128, 8]]), writes=[g3T], **cd2)
                for kk in range(3):
                    P.dma("act", cfw[:, :, kk], dap(conv_f_w, 2 * D_FF * kk, [[1, 128], [128, 48]]), writes=[cfw], **cd2)
                P.dma("act", cfb[:], dap(conv_f_b, 0, [[1, 128], [128, 48]]), writes=[cfb], **cd2)
                tot2 = P.cnt["d_const2"]
                for b in (g3T, cfw, cfb):
                    b.w = ("d_const2", tot2)

            def late_setup():
                P.op("pool", lambda e: e.memset(Umat[:], 1.0), writes=[Umat])
                P.op("pool", lambda e: e.affine_select(out=Umat[:], in_=Umat[:], pattern=[[1, 128]], compare_op=ALU.is_ge,
                                                       fill=0.0, base=0, channel_multiplier=-1), reads=[Umat], writes=[Umat])
                P.op("pool", lambda e: e.memset(Ubc[:], 1.0), writes=[Ubc])
                P.op("pool", lambda e: e.affine_select(out=Ubc[:], in_=Ubc[:], pattern=[[0, 128]], compare_op=ALU.is_ge,
                                                       fill=0.0, base=63, channel_multiplier=-1), reads=[Ubc], writes=[Ubc])
                P.op("pool", lambda e: e.memset(ones[:], 1.0), writes=[ones])

            def rg_consts():
                P.op("act", lambda e: e.activation(out=lrc[:], in_=lam[:], func=AF.Exp, scale=-1.0), reads=[lam], writes=[lrc])
                P.op("act", lambda e: e.activation(out=lrc[:], in_=lrc[:], func=AF.Ln, bias=1.0), reads=[lrc], writes=[lrc])
                P.op("dve", lambda e: e.tensor_scalar(out=lrc2[:], in0=lrc[:], scalar1=-8.0, scalar2=None, op0=ALU.mult),
                     reads=[lrc], writes=[lrc2])
                P.op("dve", lambda e: e.tensor_scalar(out=lrc[:], in0=lrc[:], scalar1=-4.0, scalar2=None, op0=ALU.mult),
                     reads=[lrc], writes=[lrc])
                P.op("dve", lambda e: e.tensor_scalar(out=hbra[:], in0=bra[:], scalar1=0.5, scalar2=None, op0=ALU.mult),
                     reads=[bra], writes=[hbra])
                P.op("dve", lambda e: e.tensor_scalar(out=hbrx[:], in0=brx[:], scalar1=0.5, scalar2=None, op0=ALU.mult),
                     reads=[brx], writes=[hbrx])

            band_idx = {}
            for blk in range(16):
                r0, r1 = 80 * blk, 80 * blk + 80
                for ci in range(r0 // 128, (r1 - 1) // 128 + 1):
                    for co in range(r0 // 128, (r1 - 1) // 128 + 1):
                        if (ci, co) not in band_idx:
                            band_idx[(ci, co)] = len(band_idx)
            NBAND = len(band_idx)
            Wgd = P.mk(177, [NBAND, 2, 128], BF16, "Wgd")

            def wgd_items():
                items = []
                for g, wsrc in ((0, w_rg_a), (1, w_rg_x)):
                    for blk in range(16):
                        r0, r1 = 80 * blk, 80 * blk + 80
                        for ci in range(r0 // 128, (r1 - 1) // 128 + 1):
                            a0, a1 = max(r0, 128 * ci), min(r1, 128 * ci + 128)
                            for co in range(r0 // 128, (r1 - 1) // 128 + 1):
                                c0, c1 = max(r0, 128 * co), min(r1, 128 * co + 128)
                                items.append(lambda g=g, wsrc=wsrc, blk=blk, ci=ci, co=co, a0=a0, a1=a1, c0=c0, c1=c1, r0=r0: P.dma(
                                    "pool", Wgd[a0 - 128 * ci:a1 - 128 * ci, band_idx[(ci, co)], g, c0 - 128 * co:c1 - 128 * co],
                                    wsrc[blk, a0 - r0:a1 - r0, c0 - r0:c1 - r0], writes=[Wgd], sem="d_wgd"))
                return items

            evac_flip = [0]

            def evac_scaled(out_ap, in_ap, scal_ap, reads, writes):
                evac_flip[0] ^= 1
                if evac_flip[0]:
                    P.op("act", lambda e: e.activation(out=out_ap, in_=in_ap, func=AF.Copy, scale=scal_ap), reads=reads, writes=writes)
                else:
                    P.op("dve", lambda e: e.tensor_scalar(out=out_ap, in0=in_ap, scalar1=scal_ap, scalar2=None, op0=ALU.mult),
                         reads=reads, writes=writes)

            def evac_copy(out_ap, in_ap, reads, writes):
                evac_flip[0] ^= 1
                if evac_flip[0]:
                    P.op("act", lambda e: e.activation(out=out_ap, in_=in_ap, func=AF.Copy), reads=reads, writes=writes)
                else:
                    P.op("dve", lambda e: e.tensor_copy(out=out_ap, in_=in_ap), reads=reads, writes=writes)

            def rms_rstd(src_ap, src_bufs, junk, ssb, rstd):
                P.op("act", lambda e: e.activation(out=junk[:], in_=src_ap, func=AF.Square, accum_out=ssb[:, 0:1]),
                     reads=src_bufs, writes=[junk, ssb])
                P.op("act", lambda e: e.activation(out=ssb[:, 1:2], in_=ssb[:, 0:1], func=AF.Sqrt, scale=1.0 / D, bias=EPS),
                     reads=[ssb], writes=[ssb])
                P.op("dve", lambda e: e.reciprocal(out=rstd[:], in_=ssb[:, 1:2]), reads=[ssb], writes=[rstd])

            const_dmas_early()
            hT = P.mk(0, [8, T], BF16, "hT", nsub=4)
            hTa = hT[0].t
            xt = [P.mk(152 + 4 * k, [D], F32, f"xt{k}") for k in range(2)] + [P.mk(36 + 4 * k, [D], F32, f"xt{2 + k}") for k in range(4)]
            xn4 = P.mk(160, [4, D], F32, "xn4")
            junk = P.mk(32, [D], BF16, "junk")
            rstd = P.sbuf("rstd", [128, 1], F32)
            ss4 = [P.sbuf(f"ss4_{k}", [128, 4], F32) for k in range(4)]
            rstd4 = [P.sbuf(f"rstd4_{k}", [128, 1], F32) for k in range(4)]

            def norm_part(get_tile):
                for j in range(4):
                    src_ap, src_bufs = get_tile(j)
                    rms_rstd(src_ap, src_bufs, junk, ss4[j], rstd4[j])
                    P.op("dve", lambda e, j=j, src_ap=src_ap: e.tensor_scalar(out=xn4[:, j, :], in0=src_ap, scalar1=rstd4[j][:, 0:1],
                                                                            scalar2=None, op0=ALU.mult),
                         reads=src_bufs + [rstd4[j]], writes=[xn4])

            def transpose_part(gT, dst_ap, dst_buf, col0, tp_banks):
                for c in range(8):
                    tp = tp_banks[c % 2]
                    for j in range(4):
                        P.op("pe", lambda e, j=j, c=c, tp=tp: e.transpose(out=tp[:, j * 128:(j + 1) * 128],
                                                                         in_=xn4[:, j, c * 128:(c + 1) * 128], identity=ident[:]),
                             reads=[xn4, ident], writes=[tp], sig=(j == 3))
                    evac_scaled(dst_ap[:, c, col0:col0 + 512], tp[:], gT[:, c:c + 1], [tp, gT], [dst_buf])

            def norm_transpose(get_tile, gT, dst_ap, dst_buf, col0, tp_banks):
                norm_part(get_tile)
                transpose_part(gT, dst_ap, dst_buf, col0, tp_banks)

            for tt in range(4):
                def get_tile(j, tt=tt):
                    i = 4 * tt + j
                    b = xt[i % 6]
                    P.dma("sp", b[:], x[i * 128:(i + 1) * 128, :], writes=[b], sem=f"d_xt{i % 6}")
                    return b[:], [b]
                norm_transpose(get_tile, g1T, hTa, hT[tt], tt * 512, ps[0:2])

            late_setup()
            const_dmas_rg()
            if debug:
                P.dma("sp", dbg["hT"], hTa, reads=hT, sem="dbg")
            y_b = P.mk(32, [8, T], BF16, "y_b", nsub=4)
            yba = y_b[0].t
            Btab = P.mk(64, [16, 8, 16], F32, "Btab")
            wqkv = [P.mk(80 + 6 * k, [8, 3, 128], BF16, f"wqkv{k}") for k in range(3)]
            qz = [P.mk(98 + 8 * k, [2, T], BF16, f"qz{k}") for k in range(2)]
            kp = [P.mk(114 + 4 * k, [T], BF16, f"kp{k}") for k in range(2)]
            Vp = [P.mk(122 + 8 * k, [16, 2, 128], BF16, f"Vp{k}") for k in range(2)]
            Pt = [P.mk(138 + k, [512], BF16, f"Pt{k}") for k in range(4)]
            rc = [P.mk(142 + 2 * k, [512], F32, f"rc{k}") for k in range(2)]
            wf = P.mk(146, [8, 16], BF16, "wf")
            zf = P.mk(147, [16, 16], F32, "zf")
            Lf = P.mk(148, [16, 16], F32, "Lf")
            cumL = P.mk(149, [16, 16], F32, "cumL")
            Cmid = P.mk(150, [16, 16], F32, "Cmid")
            Pfx = P.mk(151, [16, 16], F32, "Pfx")

            P.dma("pool", wf[:], w_in[:, O_F:O_F + 16].rearrange("(c p) n -> p c n", p=128), writes=[wf], sem="d_wf")
            for i in range(NT):
                for c in range(8):
                    P.op("pe", lambda e, i=i, c=c: e.matmul(ps[2][:, i * 16:(i + 1) * 16], lhsT=hTa[:, c, i * 128:(i + 1) * 128],
                                                           rhs=wf[:, c, :], start=(c == 0), stop=(c == 7)),
                         reads=[hT[i // 4], wf], writes=[ps[2]], sig=(c == 7))
            P.op("dve", lambda e: e.tensor_tensor(out=zf[:], in0=ps[2][:, 0:256].rearrange("p (a b) -> p a b", a=16),
                                                  in1=bfb[:].unsqueeze(1).broadcast_to([128, 16, 16]), op=ALU.add),
                 reads=[ps[2], bfb], writes=[zf])
            P.op("act", lambda e: e.activation(out=Lf[:], in_=zf[:], func=AF.Exp, scale=-1.0), reads=[zf], writes=[Lf])
            P.op("act", lambda e: e.activation(out=Lf[:], in_=Lf[:], func=AF.Ln, bias=1.0), reads=[Lf], writes=[Lf])
            Lflat = Lf[:].rearrange("p a b -> p (a b)")
            P.op("pe", lambda e: e.matmul(ps[3][:, 0:256], lhsT=Umat[:], rhs=Lflat, start=True, stop=True), reads=[Umat, Lf], writes=[ps[3]])
            P.op("pe", lambda e: e.matmul(ps[3][:, 256:512], lhsT=Ubc[:], rhs=Lflat, start=True, stop=True), reads=[Ubc, Lf], writes=[ps[3]])
            P.op("pe", lambda e: e.matmul(ps[4][:, 0:256], lhsT=ones[:], rhs=Lflat, start=True, stop=True), reads=[ones, Lf], writes=[ps[4]])
            P.op("dve", lambda e: e.memset(Pfx[:], 0.0), writes=[Pfx])
            for i in range(1, NT):
                P.op("dve", lambda e, i=i: e.tensor_tensor(out=Pfx[:, i, :], in0=Pfx[:, i - 1, :], in1=ps[4][:, (i - 1) * 16:i * 16], op=ALU.add),
                     reads=[Pfx, ps[4]], writes=[Pfx])
            P.op("dve", lambda e: e.tensor_tensor(out=cumL[:], in0=ps[3][:, 0:256].rearrange("p (a b) -> p a b", a=16), in1=Pfx[:], op=ALU.add),
                 reads=[ps[3], Pfx], writes=[cumL])
            P.op("dve", lambda e: e.tensor_tensor(out=Cmid[:], in0=ps[3][:, 256:512].rearrange("p (a b) -> p a b", a=16), in1=Pfx[:], op=ALU.add),
                 reads=[ps[3], Pfx], writes=[Cmid])
            for g in range(8):
                nj_ = 2 * g + 2
                P.op("dve", lambda e: e.tensor_tensor(out=Btab[:, :, g, 0:nj_], in0=cumL[:, 0:nj_, :].rearrange("p j h -> p h j"),
                                                      in1=Pfx[:, 2 * g + 1, :].unsqueeze(2).broadcast_to([128, 16, nj_]), op=ALU.subtract),
                     reads=[cumL, Pfx], writes=[Btab])

            def att_memset(k):
                P.op("pool", lambda e: e.memset(qz[k][:].bitcast(F32), 0.0), writes=[qz[k]])
                P.op("pool", lambda e: e.memset(Vp[k][:].bitcast(F32), ONES_BF16X2), writes=[Vp[k]])

            att_memset(0)

            S_banks = [ps[2], ps[3], ps[4]]
            O_banks = [ps[5], ps[6]]
            pj_banks = [ps[0], ps[1], ps[7]]
            cnt_s = [0]
            cnt_o = [0]
            cnt_pj = [0]
            cnt_pt = [0]
            cnt_rc = [0]
            PtH = [[P.alias(Pt[k], f"Pt{k}h{hf}") for hf in range(2)] for k in range(4)]

            def att_load(p):
                w = wqkv[p % 3]
                for wi, off in enumerate((O_Q, O_K, O_V)):
                    P.dma("pool", w[:, :, wi, :], w_in[:, off + 128 * p: off + 128 * (p + 1)].rearrange("(c p) n -> p c n", p=128),
                          writes=[w], sem=f"d_wqkv{p % 3}")

            def att_proj(p):
                w = wqkv[p % 3]
                qzb, kpb, Vpb = qz[p % 2], kp[p % 2], Vp[p % 2]
                items = []

                def qk_item(tt, wi):
                    bank = pj_banks[cnt_pj[0] % 3]
                    cnt_pj[0] += 1
                    for c in range(8):
                        P.op("pe", lambda e: e.matmul(bank[:], lhsT=w[:, c, wi, :], rhs=hTa[:, c, tt * 512:(tt + 1) * 512],
                                                      start=(c == 0), stop=(c == 7)),
                             reads=[w, hT[tt]], writes=[bank], sig=(c == 7))
                    if wi == 0:
                        P.op("dve", lambda e: e.tensor_copy(out=qzb[0:64, 0, tt * 512:(tt + 1) * 512], in_=bank[0:64, :]), reads=[bank], writes=[qzb])
                        P.op("dve", lambda e: e.tensor_copy(out=qzb[64:128, 1, tt * 512:(tt + 1) * 512], in_=bank[64:128, :]), reads=[bank], writes=[qzb])
                    else:
                        P.op("dve", lambda e: e.tensor_copy(out=kpb[:, tt * 512:(tt + 1) * 512], in_=bank[:]), reads=[bank], writes=[kpb])

                def v_item(g):
                    bank = pj_banks[cnt_pj[0] % 3]
                    cnt_pj[0] += 1
                    for jj in range(4):
                        i = 4 * g + jj
                        for c in range(8):
                            P.op("pe", lambda e: e.matmul(bank[:, jj * 128:(jj + 1) * 128], lhsT=hTa[:, c, i * 128:(i + 1) * 128],
                                                          rhs=w[:, c, 2, :], start=(c == 0), stop=(c == 7)),
                                 reads=[w, hT[g]], writes=[bank], sig=(c == 7 and jj == 3))
                    bv = bank[:].rearrange("p (a b) -> p a b", a=4)
                    P.op("dve", lambda e: e.tensor_copy(out=Vpb[:, 4 * g:4 * g + 4, 0, 0:64], in_=bv[:, :, 0:64]), reads=[bank], writes=[Vpb])
                    P.op("dve", lambda e: e.tensor_copy(out=Vpb[:, 4 * g:4 * g + 4, 1, 64:128], in_=bv[:, :, 64:128]), reads=[bank], writes=[Vpb])

                for tt in range(4):
                    for wi in range(2):
                        items.append(lambda tt=tt, wi=wi: qk_item(tt, wi))
                for g in range(4):
                    items.append(lambda g=g: v_item(g))
                return items

            pend = []
            att_load(0)
            att_load(1)
            P.op("pool", lambda e: e.memset(Wgd[:].bitcast(F32), 0.0), writes=[Wgd])
            for it_ in wgd_items():
                it_()
            wgd_q = []
            wrx = [P.mk(164.75 + 2 * k, [8, 128], BF16, f"wrx{k}") for k in range(3)]
            wrg = [P.mk(170.75 + 2 * k, [8, 128], BF16, f"wrg{k}") for k in range(3)]
            NU = 40
            def rg_load_x(u):
                co = u % 10
                w = wrx[u % 3]
                P.dma("pool", w[:], w_in[:, O_XA + 128 * co:O_XA + 128 * (co + 1)].rearrange("(c p) n -> p c n", p=128), writes=[w], sem=f"d_wrx{u % 3}")

            def rg_load_g(u):
                co = u % 10
                w = wrg[u % 3]
                P.dma("pool", w[:], w_in[:, O_GA + 128 * co:O_GA + 128 * (co + 1)].rearrange("(c p) n -> p c n", p=128), writes=[w], sem=f"d_wrg{u % 3}")

            for u in range(3):
                rg_load_x(u)
            for u in range(3):
                rg_load_g(u)
            for it_ in att_proj(0):
                it_()
            for p in range(8):
                if p + 2 < 8:
                    att_load(p + 2)
                nxt = att_proj(p + 1) if p + 1 < 8 else []
                if p == 0:
                    att_memset(1)
                    P.op("pool", lambda e: e.memset(carry[:], 0.0), writes=[carry])
                step = [0]
                qzb, kpb, Vpb = qz[p % 2], kp[p % 2], Vp[p % 2]
                for hh in range(2):
                    h = 2 * p + hh
                    orow = slice(0, 64) if hh == 0 else slice(64, 128)
                    drow = slice(64, 128) if hh == 0 else slice(0, 64)
                    for c in range(4):
                        O = O_banks[cnt_o[0] % 2]
                        cnt_o[0] += 1
                        nj = 4 * c + 4

                        def qk(j):
                            S = S_banks[cnt_s[0] % 3]
                            cnt_s[0] += 1
                            q0 = max(128 * j, 512 * c)
                            rel = q0 - 512 * c
                            P.op("pe", lambda e: e.matmul(S[:, rel:512], lhsT=kpb[:, 128 * j:128 * (j + 1)], rhs=qzb[:, hh, q0:512 * (c + 1)],
                                                          start=True, stop=True), reads=[kpb, qzb], writes=[S])
                            kpt = cnt_pt[0] % 4
                            cnt_pt[0] += 1
                            Pb = Pt[kpt]
                            used = []
                            for hf in range(2):
                                g = 2 * c + hf
                                lo = max(256 * hf, rel)
                                hi = 256 * (hf + 1)
                                if lo >= hi:
                                    continue
                                PH = PtH[kpt][hf]
                                used.append(PH)
                                P.op("act", lambda e: e.activation(out=Pb[:, lo:hi], in_=S[:, lo:hi], func=AF.Exp,
                                                                   scale=0.125, bias=Btab[:, h, g, j:j + 1]),
                                     reads=[S, Btab], writes=[PH])
                                dlo = 128 * j - 512 * c
                                if lo <= dlo < hi and j >= 4 * c:
                                    P.op("pool", lambda e: e.affine_select(out=Pb[:, dlo:dlo + 128], in_=Pb[:, dlo:dlo + 128], pattern=[[1, 128]],
                                                                           compare_op=ALU.is_ge, fill=0.0, base=0, channel_multiplier=-1),
                                         reads=[PH], writes=[PH])
                            return (j, rel, Pb, used)

                        def make_pv(item, O=O, hh=hh, nj=nj, Vpb=Vpb, p=p, c=c, orow=orow, drow=drow):
                            j, rel, Pb, used = item

                            def emit():
                                P.op("pe", lambda e: e.matmul(O[:, rel:512], lhsT=Vpb[:, j, hh, :], rhs=Pb[:, rel:512], start=(j == 0), stop=(j == nj - 1)),
                                     reads=[Vpb] + used, writes=[O], sig=(j == nj - 1))
                                if j == nj - 1:
                                    rcb = rc[cnt_rc[0] % 2]
                                    cnt_rc[0] += 1
                                    P.op("dve", lambda e: e.reciprocal(out=rcb[orow, :], in_=O[drow, :]), reads=[O], writes=[rcb])
                                    P.op("dve", lambda e: e.tensor_tensor(out=yba[orow, p, c * 512:(c + 1) * 512], in0=O[orow, :],
                                                                          in1=rcb[orow, :], op=ALU.mult),
                                         reads=[O, rcb], writes=[y_b[c]])
                            return emit

                        for j in range(nj):
                            pend.append(make_pv(qk(j)))
                            if len(pend) > 2:
                                pend.pop(0)()
                            step[0] += 1
                            if nxt and step[0] % 6 == 0:
                                nxt.pop(0)()
                while nxt:
                    nxt.pop(0)()
                for _ in range(12):
                    if wgd_q and p >= 1:
                        wgd_q.pop(0)()
            while pend:
                pend.pop(0)()
            while wgd_q:
                wgd_q.pop(0)()
            rg_consts()

            if debug:
                P.dma("sp", dbg["y_b"], yba, reads=y_b, sem="dbg")
                P.dma("sp", dbg["Btab"], Btab[:], reads=[Btab], sem="dbg")
            y_a = P.mk(64, [10, T], BF16, "y_a", nsub=4)
            yaa = y_a[0].t
            o = 128
            xa_sb = [P.mk(o + 2.25 * k, [576], F32, f"xa_sb{k}") for k in range(3)]
            o += 6.75
            xc = [P.mk(o + 2 * k, [512], F32, f"xc{k}") for k in range(5)] + [P.mk(120 + 2 * k, [512], F32, f"xc{5 + k}") for k in range(2)]
            o += 10
            xcb = [P.mk(o + k, [512], BF16, f"xcb{k}") for k in range(4)]
            o += 4
            rb = [P.mk(o + 2 * k, [512], F32, f"rb{k}") for k in range(2)] + [P.mk(104 + 2 * k, [512], F32, f"rb{2 + k}") for k in range(2)]
            o += 4
            igb = [P.mk(o + 2 * k, [512], F32, f"igb{k}") for k in range(2)] + [P.mk(108 + 2 * k, [512], F32, f"igb{2 + k}") for k in range(2)]
            o += 4
            a2b = [P.mk(o + 2 * k, [512], F32, f"a2b{k}") for k in range(2)] + [P.mk(112 + 2 * k, [512], F32, f"a2b{2 + k}") for k in range(2)]
            o += 4
            ggb = [P.mk(o + 2 * k, [512], F32, f"ggb{k}") for k in range(2)] + [P.mk(116 + 2 * k, [512], F32, f"ggb{2 + k}") for k in range(2)]
            o += 4
            xh = P.mk(190.5, [10, 3], F32, "xh")
            hcar = P.mk(190.75, [10], F32, "hcar")
            P.op("dve", lambda e: e.memset(xh[:], 0.0), writes=[xh])
            P.op("dve", lambda e: e.memset(hcar[:], 0.0), writes=[hcar])
            NU = 40

            def rg_A1(u):
                tt, co = divmod(u, 10)
                w = wrx[u % 3]
                xab, xcc, xcbb = xa_sb[u % 3], xc[u % 7], xcb[u % 4]
                tsl = slice(tt * 512, (tt + 1) * 512)
                pa_ = ps[u % 2]
                for c in range(8):
                    P.op("pe", lambda e: e.matmul(pa_[:], lhsT=w[:, c, :], rhs=hTa[:, c, tsl], start=(c == 0), stop=(c == 7)),
                         reads=[w, hT[tt]], writes=[pa_], sig=(c == 7))
                if u + 3 < NU:
                    rg_load_x(u + 3)
                P.op("act", lambda e: e.activation(out=xab[:, 0:3], in_=xh[:, co, :], func=AF.Copy), reads=[xh], writes=[xab])
                P.op("act", lambda e: e.activation(out=xab[:, 3:515], in_=pa_[:], func=AF.Copy), reads=[pa_], writes=[xab])
                P.op("act", lambda e: e.activation(out=xh[:, co, :], in_=pa_[:, 509:512], func=AF.Copy), reads=[pa_], writes=[xh])
                P.op("dve", lambda e: e.tensor_scalar(out=xcc[:], in0=xab[:, 0:512], scalar1=caw[:, co, 0:1], scalar2=cab[:, co:co + 1],
                                                      op0=ALU.mult, op1=ALU.add), reads=[xab, caw, cab], writes=[xcc])
                for kk in range(1, 4):
                    P.op("dve", lambda e: e.scalar_tensor_tensor(out=xcc[:], in0=xab[:, kk:kk + 512], scalar=caw[:, co, kk:kk + 1],
                                                                 in1=xcc[:], op0=ALU.mult, op1=ALU.add),
                         reads=[xab, caw, xcc], writes=[xcc])
                P.op("dve", lambda e: e.tensor_copy(out=xcbb[:], in_=xcc[:]), reads=[xcc], writes=[xcbb])

            def rg_A2(u):
                tt, co = divmod(u, 10)
                w = wrg[u % 3]
                tsl = slice(tt * 512, (tt + 1) * 512)
                s3 = 2 + 3 * (u % 2)
                pg_, pr_, pi_ = ps[s3], ps[s3 + 1], ps[s3 + 2]
                nb = [ci for ci in (co - 1, co, co + 1) if (ci, co) in band_idx]
                for g, pb_ in ((0, pr_), (1, pi_)):
                    for n_, ci in enumerate(nb):
                        xsrc = xcb[(u + ci - co) % 4]
                        P.op("pe", lambda e: e.matmul(pb_[:], lhsT=Wgd[:, band_idx[(ci, co)], g, :], rhs=xsrc[:], start=(n_ == 0), stop=(n_ == len(nb) - 1)),
                             reads=[Wgd, xsrc], writes=[pb_], sig=(n_ == len(nb) - 1))
                for c in range(8):
                    P.op("pe", lambda e: e.matmul(pg_[:], lhsT=w[:, c, :], rhs=hTa[:, c, tsl], start=(c == 0), stop=(c == 7)),
                         reads=[w, hT[tt]], writes=[pg_], sig=(c == 7))
                if u + 3 < NU:
                    rg_load_g(u + 3)

            def rg_bufs(u):
                k4 = u % 4
                s3 = 2 + 3 * (u % 2)
                return xc[u % 7], rb[k4], igb[k4], a2b[k4], ggb[k4], ps[s3], ps[s3 + 1], ps[s3 + 2]

            def rg_BG(u):
                co = u % 10
                xcc, rbb, igbb, a2bb, gg, pg_, pr_, pi_ = rg_bufs(u)
                P.op("act", lambda e: e.activation(out=gg[:], in_=pg_[:], func=AF.Gelu_apprx_tanh), reads=[pg_], writes=[gg])
                P.op("act", lambda e: e.activation(out=rbb[:], in_=pr_[:], func=AF.Tanh, scale=0.5, bias=hbra[:, co:co + 1]), reads=[pr_, hbra], writes=[rbb])
                P.op("act", lambda e: e.activation(out=igbb[:], in_=pi_[:], func=AF.Tanh, scale=0.5, bias=hbrx[:, co:co + 1]), reads=[pi_, hbrx], writes=[igbb])

            def rg_BE(u):
                co = u % 10
                xcc, rbb, igbb, a2bb, gg, pg_, pr_, pi_ = rg_bufs(u)
                P.op("act", lambda e: e.activation(out=a2bb[:], in_=rbb[:], func=AF.Exp, scale=lrc2[:, co:co + 1], bias=lrc2[:, co:co + 1]), reads=[rbb, lrc2], writes=[a2bb])
                P.op("act", lambda e: e.activation(out=rbb[:], in_=rbb[:], func=AF.Exp, scale=lrc[:, co:co + 1], bias=lrc[:, co:co + 1]), reads=[rbb, lrc], writes=[rbb])
                P.op("act", lambda e: e.activation(out=a2bb[:], in_=a2bb[:], func=AF.Ln, scale=-1.0, bias=1.0), reads=[a2bb], writes=[a2bb])
                P.op("act", lambda e: e.activation(out=a2bb[:], in_=a2bb[:], func=AF.Exp, scale=0.5), reads=[a2bb], writes=[a2bb])

            def rg_BD(u):
                tt, co = divmod(u, 10)
                xcc, rbb, igbb, a2bb, gg, pg_, pr_, pi_ = rg_bufs(u)
                tsl = slice(tt * 512, (tt + 1) * 512)
                P.op("dve", lambda e: e.scalar_tensor_tensor(out=igbb[:], in0=igbb[:], scalar=1.0, in1=xcc[:], op0=ALU.add, op1=ALU.mult),
                     reads=[igbb, xcc], writes=[igbb])
                P.op("dve", lambda e: e.scalar_tensor_tensor(out=igbb[:], in0=igbb[:], scalar=0.5, in1=a2bb[:], op0=ALU.mult, op1=ALU.mult),
                     reads=[igbb, a2bb], writes=[igbb])
                P.op("dve", lambda e: e.tensor_tensor_scan(out=a2bb[:], data0=rbb[:], data1=igbb[:], initial=hcar[:, co:co + 1],
                                                           op0=ALU.mult, op1=ALU.add), reads=[rbb, igbb, hcar], writes=[a2bb])
                P.op("dve", lambda e: e.tensor_copy(out=hcar[:, co:co + 1], in_=a2bb[:, 511:512]), reads=[a2bb], writes=[hcar])
                P.op("dve", lambda e: e.tensor_tensor(out=yaa[:, co, tsl], in0=gg[:], in1=a2bb[:], op=ALU.mult),
                     reads=[gg, a2bb], writes=[y_a[tt]])

            rg_A1(0)
            rg_A1(1)
            rg_A1(2)
            rg_A2(0)
            rg_A2(1)
            for u in range(0, NU, 2):
                for v in (u + 3, u + 4):
                    if v < NU:
                        rg_A1(v)
                rg_BG(u)
                rg_BG(u + 1)
                for v in (u + 2, u + 3):
                    if v < NU:
                        rg_A2(v)
                rg_BE(u)
                rg_BE(u + 1)
                if u >= 2:
                    rg_BD(u - 2)
                    rg_BD(u - 1)
            rg_BD(NU - 2)
            rg_BD(NU - 1)

            if debug:
                P.dma("sp", dbg["y_a"], yaa, reads=y_a, sem="dbg")
            merged = P.mk(128, [8, T], BF16, "merged", nsub=4)
            mga = merged[0].t
            wb_g = [P.mk(o_, [8, 2, 128], BF16, f"wb_g{k}") for k, o_ in enumerate((165, 177))]
            wb_a = [P.mk(o_, [10, 128], BF16, f"wb_a{k}") for k, o_ in enumerate((169, 181))]
            wb_b = [P.mk(o_, [8, 128], BF16, f"wb_b{k}") for k, o_ in enumerate((171.5, 183.5))]
            sga = [P.mk(o_, [512], F32, f"sga{k}") for k, o_ in enumerate((160, 187.5))]
            sgb = [P.mk(o_, [512], F32, f"sgb{k}") for k, o_ in enumerate((162, 189.5))]
            mab = [P.mk(o_, [512], F32, f"mab{k}") for k, o_ in enumerate((185.5, 124))]

            def b_load(co):
                k2 = co % 2
                csl = slice(128 * co, 128 * (co + 1))
                P.dma("pool", wb_g[k2][:, :, 0, :], w_in[:, O_GMA + 128 * co:O_GMA + 128 * (co + 1)].rearrange("(c p) n -> p c n", p=128), writes=[wb_g[k2]], sem=f"d_wbg{k2}")
                P.dma("pool", wb_g[k2][:, :, 1, :], w_in[:, O_GMB + 128 * co:O_GMB + 128 * (co + 1)].rearrange("(c p) n -> p c n", p=128), writes=[wb_g[k2]], sem=f"d_wbg{k2}")
                P.dma("pool", wb_a[k2][:], w_proj_a[:, csl].rearrange("(c p) n -> p c n", p=128), writes=[wb_a[k2]], sem=f"d_wba{k2}")
                P.dma("pool", wb_b[k2][:], w_proj_b[:, csl].rearrange("(c p) n -> p c n", p=128), writes=[wb_b[k2]], sem=f"d_wbb{k2}")

            b_load(0)
            wout = P.mk(104, [8, D], BF16, "wout")
            g2b = P.mk(120, [D], F32, "g2b")
            P.dma("sp", g2b[:], dap(mix_norm_post, 0, [[0, 128], [1, D]]), writes=[g2b], sem="d_g2b")
            it = 0
            for co in range(8):
                k2 = co % 2
                wg_, wa_, wb_ = wb_g[k2], wb_a[k2], wb_b[k2]
                for tt in range(4):
                    if tt == 1:
                        if co + 1 < 8:
                            b_load(co + 1)
                        if co == 4:
                            const_dmas_ffn()
                        if co == 2:
                            for c2 in range(2):
                                P.dma("pool", wout[:, 4 * c2:4 * c2 + 4, :], w_out[512 * c2:512 * (c2 + 1), :].rearrange("(c p) n -> p c n", p=128), writes=[wout], sem="d_wout")
                    s2 = it % 2
                    it += 1
                    tsl = slice(tt * 512, (tt + 1) * 512)
                    pga, pua, pgb, pub = ps[0 + 4 * s2], ps[1 + 4 * s2], ps[2 + 4 * s2], ps[3 + 4 * s2]
                    sa, sb, ma = sga[s2], sgb[s2], mab[s2]
                    for c in range(8):
                        P.op("pe", lambda e: e.matmul(pga[:], lhsT=wg_[:, c, 0, :], rhs=hTa[:, c, tsl], start=(c == 0), stop=(c == 7)),
                             reads=[wg_, hT[tt]], writes=[pga], sig=(c == 7))
                    for c in range(8):
                        P.op("pe", lambda e: e.matmul(pgb[:], lhsT=wg_[:, c, 1, :], rhs=hTa[:, c, tsl], start=(c == 0), stop=(c == 7)),
                             reads=[wg_, hT[tt]], writes=[pgb], sig=(c == 7))
                    for bb in range(10):
                        P.op("pe", lambda e: e.matmul(pua[:], lhsT=wa_[:, bb, :], rhs=yaa[:, bb, tsl], start=(bb == 0), stop=(bb == 9)),
                             reads=[wa_, y_a[tt]], writes=[pua], sig=(bb == 9))
                    for c in range(8):
                        P.op("pe", lambda e: e.matmul(pub[:], lhsT=wb_[:, c, :], rhs=yba[:, c, tsl], start=(c == 0), stop=(c == 7)),
                             reads=[wb_, y_b[tt]], writes=[pub], sig=(c == 7))
                    P.op("act", lambda e: e.activation(out=sa[:], in_=pga[:], func=AF.Sigmoid, bias=bmg[:, 0, co:co + 1]), reads=[pga, bmg], writes=[sa])
                    P.op("act", lambda e: e.activation(out=sb[:], in_=pgb[:], func=AF.Sigmoid, bias=bmg[:, 1, co:co + 1]), reads=[pgb, bmg], writes=[sb])
                    P.op("dve", lambda e: e.tensor_tensor(out=ma[:], in0=pua[:], in1=sa[:], op=ALU.mult), reads=[pua, sa], writes=[ma])
                    P.op("dve", lambda e: e.tensor_tensor(out=sb[:], in0=pub[:], in1=sb[:], op=ALU.mult), reads=[pub, sb], writes=[sb])
                    P.op("dve", lambda e: e.tensor_tensor(out=mga[:, co, tsl], in0=ma[:], in1=sb[:], op=ALU.add),
                         reads=[ma, sb], writes=[merged[tt]])
            if debug:
                P.dma("sp", dbg["merged"], mga, reads=merged, sem="dbg")
            wu = [P.mk(64 + 4 * k, [8, 2, 128], BF16, f"wu{k}") for k in range(4)]
            def d_load(g):
                n = g % 24
                w = wu[g % 4]
                P.dma("pool", w[:, :, 0, :], w_up[:, 128 * n:128 * (n + 1)].rearrange("(c p) n -> p c n", p=128), writes=[w], sem=f"d_wu{g % 4}")
                P.dma("pool", w[:, :, 1, :], w_up[:, D_FF + 128 * n:D_FF + 128 * (n + 1)].rearrange("(c p) n -> p c n", p=128), writes=[w], sem=f"d_wu{g % 4}")

            x1 = P.mk(0, [NT, D], F32, "x1", nsub=NT)
            x1a = x1[0].t
            g4b = P.mk(84, [D], F32, "g4b")
            xt = [P.mk(88 + 4 * k, [D], F32, f"xtc{k}") for k in range(2)]
            tmp = P.mk(96, [D], F32, "tmp")
            junk = P.mk(100, [D], BF16, "junkc")
            ssc = P.sbuf("ssc", [128, 4], F32)
            rstd2 = P.sbuf("rstd2", [128, 1], F32)
            P.dma("sp", g4b[:], dap(ffn_norm_post, 0, [[0, 128], [1, D]]), writes=[g4b], sem="d_g4b")

            def post_norm_residual(pbanks, res_ap, res_bufs, gb, out_ap, out_bufs):
                for hf in range(2):
                    P.op("act", lambda e, hf=hf: e.activation(out=junk[:, hf * 512:(hf + 1) * 512], in_=pbanks[hf][:], func=AF.Square, accum_out=ssc[:, hf:hf + 1]),
                         reads=[pbanks[hf]], writes=[junk, ssc])
                P.op("dve", lambda e: e.tensor_tensor(out=ssc[:, 2:3], in0=ssc[:, 0:1], in1=ssc[:, 1:2], op=ALU.add), reads=[ssc], writes=[ssc])
                P.op("act", lambda e: e.activation(out=ssc[:, 3:4], in_=ssc[:, 2:3], func=AF.Sqrt, scale=1.0 / D, bias=EPS), reads=[ssc], writes=[ssc])
                P.op("dve", lambda e: e.reciprocal(out=rstd2[:], in_=ssc[:, 3:4]), reads=[ssc], writes=[rstd2])
                for hf in range(2):
                    hs_ = slice(hf * 512, (hf + 1) * 512)
                    P.op("dve", lambda e, hf=hf, hs_=hs_: e.scalar_tensor_tensor(out=tmp[:, hs_], in0=pbanks[hf][:], scalar=rstd2[:, 0:1], in1=gb[:, hs_],
                                                                              op0=ALU.mult, op1=ALU.mult), reads=[pbanks[hf], rstd2, gb], writes=[tmp])
                P.op("dve", lambda e: e.tensor_tensor(out=out_ap, in0=tmp[:], in1=res_ap, op=ALU.add), reads=[tmp] + res_bufs, writes=out_bufs)

            for i in range(NT):
                bk = [ps[2 * (i % 4)], ps[2 * (i % 4) + 1]]
                for hf in range(2):
                    for c in range(8):
                        P.op("pe", lambda e, c=c, i=i, hf=hf, bk=bk: e.matmul(bk[hf][:], lhsT=mga[:, c, i * 128:(i + 1) * 128], rhs=wout[:, c, hf * 512:(hf + 1) * 512],
                                                                           start=(c == 0), stop=(c == 7)), reads=[merged[i // 4], wout], writes=[bk[hf]], sig=(c == 7))
                xb_ = xt[i % 2]
                P.dma("sp", xb_[:], x[i * 128:(i + 1) * 128, :], writes=[xb_], sem=f"d_xtc{i % 2}")
                post_norm_residual(bk, xb_[:], [xb_], g2b, x1a[:, i, :], [x1[i]])
                if i == 5:
                    for g in range(3):
                        d_load(g)
                    junk_c = junk
                    xn4 = P.mk(160, [4, D], F32, "xn4c")
                    junk = P.mk(176, [D], BF16, "junkn")
                    norm_part(lambda j: (x1a[:, j, :], [x1[j]]))
                    junk = junk_c

            if debug:
                P.dma("sp", dbg["x1"], x1a, reads=x1, sem="dbg")
            wd = P.mk(112, [24, D], BF16, "wd")
            act_s = P.mk(160, [24, 512], BF16, "act", nsub=6)
            act = act_s[0].t
            h2T = P.mk(184, [8, 512], BF16, "h2T")
            tmp = P.mk(80, [D], F32, "tmpd")
            junk2 = P.mk(109, [D], BF16, "junkd2")

            def ffn_get_tile(tc):
                def get_tile(j):
                    i = 4 * tc + j
                    return x1a[:, i, :], [x1[i]]
                return get_tile

            transpose_part(g3T, h2T.t, h2T, 0, ps[0:2])
            for c6 in range(6):
                P.dma("pool", wd[:, 4 * c6:4 * c6 + 4, :], w_down[512 * c6:512 * (c6 + 1), :].rearrange("(c p) n -> p c n", p=128), writes=[wd], sem="d_wd")
            for tc in range(4):
                ug = [P.mk(88 + 2.25 * k, [576], F32, f"ug{k}") for k in range(2)]
                uv = [P.mk(92.5 + 2.25 * k, [576], F32, f"uv{k}") for k in range(2)]
                cvg = [P.mk(97 + 2 * k, [512], F32, f"cvg{k}") for k in range(3)]
                cvv = [P.mk(103 + 2 * k, [512], F32, f"cvv{k}") for k in range(3)]

                def d_A(n):
                    g = 24 * tc + n
                    w = wu[g % 4]
                    if g + 3 < 96:
                        d_load(g + 3)
                    k2 = n % 2
                    for wi, (ub, cv) in enumerate(((ug[k2], cvg[n % 3]), (uv[k2], cvv[n % 3]))):
                        ch = n + 24 * wi
                        bank = ps[2 + (2 * n + wi) % 4]
                        for c in range(8):
                            P.op("pe", lambda e: e.matmul(bank[:], lhsT=w[:, c, wi, :], rhs=h2T[:, c, :], start=(c == 0), stop=(c == 7)),
                                 reads=[w, h2T], writes=[bank], sig=(c == 7))
                        P.op("act", lambda e: e.activation(out=ub[:, 0:2], in_=carry[:, ch, :], func=AF.Copy), reads=[carry], writes=[ub])
                        P.op("act", lambda e: e.activation(out=ub[:, 2:514], in_=bank[:], func=AF.Copy), reads=[bank], writes=[ub])
                        P.op("act", lambda e: e.activation(out=cv[:], in_=bank[:], func=AF.Identity, scale=cfw[:, ch, 2:3], bias=cfb[:, ch:ch + 1]),
                             reads=[bank, cfw, cfb], writes=[cv])
                        P.op("act", lambda e: e.activation(out=carry[:, ch, :], in_=bank[:, 510:512], func=AF.Copy), reads=[bank], writes=[carry])

                def d_B(n):
                    k2 = n % 2
                    for wi, (ub, cv) in enumerate(((ug[k2], cvg[n % 3]), (uv[k2], cvv[n % 3]))):
                        ch = n + 24 * wi
                        for kk in range(2):
                            P.op("dve", lambda e: e.scalar_tensor_tensor(out=cv[:], in0=ub[:, kk:kk + 512], scalar=cfw[:, ch, kk:kk + 1],
                                                                         in1=cv[:], op0=ALU.mult, op1=ALU.add),
                                 reads=[ub, cfw, cv], writes=[cv])
                        if wi == 0:
                            P.op("act", lambda e: e.activation(out=cv[:], in_=cv[:], func=AF.Gelu_apprx_tanh), reads=[cv], writes=[cv])

                def d_M(n):
                    P.op("dve", lambda e: e.tensor_tensor(out=act[:, n, :], in0=cvg[n % 3][:], in1=cvv[n % 3][:], op=ALU.mult),
                         reads=[cvg[n % 3], cvv[n % 3]], writes=[act_s[n // 4]])

                d_A(0)
                for n in range(24):
                    if n + 1 < 24:
                        d_A(n + 1)
                    d_B(n)
                    if n >= 1:
                        d_M(n - 1)
                d_M(23)
                if tc + 1 < 4:
                    xn4 = P.mk(88, [4, D], F32, "xn4d")
                    junk = P.mk(109, [D], BF16, "junkd")
                    norm_part(ffn_get_tile(tc + 1))
                junk_save = junk
                junk = junk2
                for j in range(4):
                    i = 4 * tc + j
                    bk = [ps[2 + 2 * (j % 2)], ps[3 + 2 * (j % 2)]]
                    for hf in range(2):
                        for kc in range(24):
                            P.op("pe", lambda e: e.matmul(bk[hf][:], lhsT=act[:, kc, j * 128:(j + 1) * 128], rhs=wd[:, kc, hf * 512:(hf + 1) * 512],
                                                          start=(kc == 0), stop=(kc == 23)), reads=[act_s[kc // 4], wd], writes=[bk[hf]], sig=(kc == 23))
                    post_norm_residual(bk, x1a[:, i, :], [x1[i]], g4b, x1a[:, i, :], [x1[i]])
                    P.dma("sp", y[i * 128:(i + 1) * 128, :], x1a[:, i, :], reads=[x1[i]], sem=f"st{j % 2}")
                junk = junk_save
                if tc + 1 < 4:
                    transpose_part(g3T, h2T.t, h2T, 0, ps[0:2])
            P.wait_all("sp", ["st0", "st1", "dbg"])
        P.run(program)
    return nc


_NC_CACHE = {}


def kernel(**inputs):
    n = 8
    if "nc" not in _NC_CACHE:
        _NC_CACHE["nc"] = build()
    nc = _NC_CACHE["nc"]
    f = lambda a: np.ascontiguousarray(np.asarray(a, dtype=np.float32))
    xs = f(inputs["x"])
    shared = {
        "mix_norm_pre": f(inputs["mix_norm_pre"]).reshape(D),
        "mix_norm_post": f(inputs["mix_norm_post"]).reshape(D),
        "w_in": f(inputs["w_in"]).reshape(D, D_IN),
        "conv_a_w": f(inputs["conv_a_w"]).reshape(4, D_RNN),
        "conv_a_b": f(inputs["conv_a_b"]).reshape(D_RNN),
        "w_rg_a": f(inputs["w_rg_a"]).reshape(16, 80, 80),
        "b_rg_a": f(inputs["b_rg_a"]).reshape(D_RNN),
        "w_rg_x": f(inputs["w_rg_x"]).reshape(16, 80, 80),
        "b_rg_x": f(inputs["b_rg_x"]).reshape(D_RNN),
        "lru_lambda": f(inputs["lru_lambda"]).reshape(D_RNN),
        "b_forget": f(inputs["b_forget"]).reshape(16),
        "b_merge": f(inputs["b_merge"]).reshape(2, D),
        "w_proj_a": f(inputs["w_proj_a"]).reshape(D_RNN, D),
        "w_proj_b": f(inputs["w_proj_b"]).reshape(D_ATT, D),
        "w_out": f(inputs["w_out"]).reshape(D, D),
        "ffn_norm_pre": f(inputs["ffn_norm_pre"]).reshape(D),
        "ffn_norm_post": f(inputs["ffn_norm_post"]).reshape(D),
        "w_up": f(inputs["w_up"]).reshape(D, 2 * D_FF),
        "conv_f_w": f(inputs["conv_f_w"]).reshape(3, 2 * D_FF),
        "conv_f_b": f(inputs["conv_f_b"]).reshape(2 * D_FF),
        "w_down": f(inputs["w_down"]).reshape(D_FF, D),
    }
    in_maps = [dict(shared, x=xs[b]) for b in range(n)]
    res = run_bass_kernel_spmd(nc, in_maps, core_ids=list(range(n)))
    return np.stack([np.asarray(r["y"], dtype=np.float32) for r in res.results], axis=0)
```

```python
import numpy as np
import concourse.bass as bass
import concourse.mybir as mybir
from concourse.bass_utils import run_bass_kernel_spmd
from contextlib import ExitStack
import struct

F32 = mybir.dt.float32
BF16 = mybir.dt.bfloat16
AF = mybir.ActivationFunctionType
ALU = mybir.AluOpType

T = 2048
D = 1024
NT = 16
D_RNN = 1280
D_ATT = 1024
D_FF = 3072
D_IN = 7696
O_XA, O_GA, O_Q, O_K, O_V, O_F, O_GMA, O_GMB = 0, 1280, 2560, 3584, 4608, 5632, 5648, 6672
EPS = 1e-6
ONES_BF16X2 = struct.unpack("<f", struct.pack("<I", 0x3F803F80))[0]
KB = 256
ARENA_KB = 192


class Buf:
    def __init__(self, t, name, excl=False, rng=None):
        self.t = t
        self.name = name
        self.w = None
        self.r = []
        self.excl = excl
        self.rng = rng

    def __getitem__(self, idx):
        return self.t[idx]


class Prog:
    ENGS = ["pe", "act", "dve", "pool", "sp"]

    def __init__(self, nc, stack):
        self.nc = nc
        self.stack = stack
        self.sems = {}
        self.res = {}
        self.arena = None
        self.cur = None
        self.e = None
        self.needed = {}
        self.rank = None
        self.reset()
        for k in ["pe", "act", "dve", "pool"]:
            self.newsem(k)

    def reset(self):
        self.cnt = {k: 0 for k in self.sems}
        self.seen = {k: {} for k in self.ENGS}
        self.abufs = []

    def newsem(self, key):
        if key not in self.sems:
            self.sems[key] = self.stack.enter_context(self.nc.semaphore("s_" + key))
        self.cnt.setdefault(key, 0)

    def sbuf(self, name, shape, dt):
        if name not in self.res:
            self.res[name] = self.stack.enter_context(self.nc.sbuf_tensor(name, list(shape), dt))
        return Buf(self.res[name], name)

    def psum(self, name, shape, dt):
        if name not in self.res:
            self.res[name] = self.stack.enter_context(self.nc.psum_tensor(name, list(shape), dt))
        return Buf(self.res[name], name, excl=True)

    def mk(self, offkb, shape, dt, name, nsub=1):
        n = int(np.prod(shape))
        words = (n + 1) // 2 if dt == BF16 else n
        off = int(round(offkb * KB))
        assert off + words <= ARENA_KB * KB, (name, offkb, words)
        ap = self.arena[:, off:off + words]
        if dt == BF16:
            ap = ap.bitcast(BF16)
        if len(shape) == 2:
            ap = ap.rearrange("p (a b) -> p a b", a=shape[0])
        elif len(shape) == 3:
            ap = ap.rearrange("p (a b c) -> p a b c", a=shape[0], b=shape[1])
        elif len(shape) == 4:
            ap = ap.rearrange("p (a b c d) -> p a b c d", a=shape[0], b=shape[1], c=shape[2])
        deps = []
        keep = []
        for ob in self.abufs:
            s, e = ob.rng
            if s < off + words and off < e:
                if ob.w is not None:
                    deps.append(ob.w)
                deps.extend(ob.r)
            keep.append(ob)
        self.abufs = keep
        out = []
        for i in range(nsub):
            b = Buf(ap, f"{name}{i}", rng=(off, off + words))
            b.r = list(deps)
            self.abufs.append(b)
            out.append(b)
        return out[0] if nsub == 1 else out

    def alias(self, buf, name):
        b = Buf(buf.t, name, excl=buf.excl, rng=buf.rng)
        b.w = buf.w
        b.r = list(buf.r)
        self.abufs.append(b)
        return b

    def _deps(self, reads, writes, eng, skip=None, dma=False):
        need = {}

        def add(d, allow_same):
            if d is None:
                return
            k, v = d
            if k == skip:
                return
            if k == eng and not allow_same and not dma:
                return
            if need.get(k, 0) < v:
                need[k] = v

        for b in reads:
            add(b.w, True)
            if b.excl:
                for d in b.r:
                    add(d, False)
        for b in writes:
            add(b.w, False)
            for d in b.r:
                add(d, False)
        waits = []
        for k, v in need.items():
            if self.seen[eng].get(k, 0) < v:
                self.seen[eng][k] = v
                waits.append((k, v))
        return waits

    CE = ("pe", "act", "dve", "pool")

    def _note(self, waits):
        if self.rank is None:
            for k, v in waits:
                if k in self.CE:
                    self.needed.setdefault(k, set()).add(v)

    def _wv(self, k, v):
        return self.rank[k][v] if k in self.CE else v

    def op(self, eng, fn, reads=(), writes=(), sig=True):
        waits = self._deps(reads, writes, eng)
        self._note(waits)
        if sig:
            self.cnt[eng] += 1
            v = self.cnt[eng]
        else:
            v = self.cnt[eng] + 1
        for b in reads:
            b.r.append((eng, v))
        for b in writes:
            b.w = (eng, v)
            b.r = []
        if self.cur == eng:
            e = self.e
            for k, val in waits:
                e.wait_ge(self.sems[k], self._wv(k, val))
            ins = fn(e)
            if sig and v in self.rank[eng]:
                ins.then_inc(self.sems[eng], 1)

    def dma(self, q, out_ap, in_ap, reads=(), writes=(), sem=None, **kw):
        self.newsem(sem)
        waits = self._deps(reads, writes, q, skip=sem, dma=True)
        self._note(waits)
        self.cnt[sem] += 16
        v = self.cnt[sem]
        for b in reads:
            b.r.append((sem, v))
        for b in writes:
            b.w = (sem, v)
            b.r = []
        if self.cur == q:
            e = self.e
            for k, val in waits:
                e.wait_ge(self.sems[k], self._wv(k, val))
            e.dma_start(out=out_ap, in_=in_ap, **kw).then_inc(self.sems[sem], 16)

    def wait_all(self, q, keys):
        if self.cur == q:
            for k in keys:
                if self.cnt.get(k, 0) > 0:
                    self.e.wait_ge(self.sems[k], self.cnt[k])

    def run(self, program):
        def one(eng, e):
            self.cur, self.e = eng, e
            self.reset()
            program()

        one(None, None)
        self.rank = {k: {v: i + 1 for i, v in enumerate(sorted(self.needed.get(k, ())))} for k in self.CE}
        with self.nc.Block() as block:
            @block.tensor
            def _(e):
                one("pe", e)

            @block.scalar
            def _(e):
                one("act", e)

            @block.vector
            def _(e):
                one("dve", e)

            @block.gpsimd
            def _(e):
                one("pool", e)

            @block.sync
            def _(e):
                one("sp", e)


def build(debug=False):
    nc = bass.Bass("TRN2", target_bir_lowering=False)

    def din(name, shape):
        return nc.dram_tensor(name, list(shape), F32, kind="ExternalInput").ap()

    x = din("x", [T, D])
    mix_norm_pre = din("mix_norm_pre", [D])
    mix_norm_post = din("mix_norm_post", [D])
    w_in = din("w_in", [D, D_IN])
    conv_a_w = din("conv_a_w", [4, D_RNN])
    conv_a_b = din("conv_a_b", [D_RNN])
    w_rg_a = din("w_rg_a", [16, 80, 80])
    b_rg_a = din("b_rg_a", [D_RNN])
    w_rg_x = din("w_rg_x", [16, 80, 80])
    b_rg_x = din("b_rg_x", [D_RNN])
    lru_lambda = din("lru_lambda", [D_RNN])
    b_forget = din("b_forget", [16])
    b_merge = din("b_merge", [2, D])
    w_proj_a = din("w_proj_a", [D_RNN, D])
    w_proj_b = din("w_proj_b", [D_ATT, D])
    w_out = din("w_out", [D, D])
    ffn_norm_pre = din("ffn_norm_pre", [D])
    ffn_norm_post = din("ffn_norm_post", [D])
    w_up = din("w_up", [D, 2 * D_FF])
    conv_f_w = din("conv_f_w", [3, 2 * D_FF])
    conv_f_b = din("conv_f_b", [2 * D_FF])
    w_down = din("w_down", [D_FF, D])
    y = nc.dram_tensor("y", [T, D], F32, kind="ExternalOutput").ap()
    dbg = {}
    if debug:
        dbg["hT"] = nc.dram_tensor("dbg_hT", [128, 8, T], BF16, kind="ExternalOutput").ap()
        dbg["y_b"] = nc.dram_tensor("dbg_y_b", [128, 8, T], BF16, kind="ExternalOutput").ap()
        dbg["y_a"] = nc.dram_tensor("dbg_y_a", [128, 10, T], BF16, kind="ExternalOutput").ap()
        dbg["merged"] = nc.dram_tensor("dbg_merged", [128, 8, T], BF16, kind="ExternalOutput").ap()
        dbg["x1"] = nc.dram_tensor("dbg_x1", [128, NT, D], F32, kind="ExternalOutput").ap()
        dbg["Btab"] = nc.dram_tensor("dbg_Btab", [128, 16, 8, 16], F32, kind="ExternalOutput").ap()

    def dap(t, off, pat):
        return bass.AP(t.tensor, off, pat)

    with ExitStack() as st:
        P = Prog(nc, st)
        arena_t = st.enter_context(nc.sbuf_tensor("arena", [128, ARENA_KB * KB], F32))
        P.arena = arena_t

        def program():
            ident = P.sbuf("ident", [128, 128], F32)
            Umat = P.sbuf("Umat", [128, 128], F32)
            Ubc = P.sbuf("Ubc", [128, 128], F32)
            ones = P.sbuf("ones", [128, 128], F32)
            caw = P.sbuf("caw", [128, 10, 4], F32)
            cab = P.sbuf("cab", [128, 10], F32)
            bra = P.sbuf("bra", [128, 10], F32)
            brx = P.sbuf("brx", [128, 10], F32)
            lam = P.sbuf("lam", [128, 10], F32)
            lrc = P.sbuf("lrc", [128, 10], F32)
            lrc2 = P.sbuf("lrc2", [128, 10], F32)
            hbra = P.sbuf("hbra", [128, 10], F32)
            hbrx = P.sbuf("hbrx", [128, 10], F32)
            g1T = P.sbuf("g1T", [128, 8], F32)
            g3T = P.sbuf("g3T", [128, 8], F32)
            bmg = P.sbuf("bmg", [128, 2, 8], F32)
            cfw = P.sbuf("cfw", [128, 48, 3], F32)
            cfb = P.sbuf("cfb", [128, 48], F32)
            bfb = P.sbuf("bfb", [128, 16], F32)
            carry = P.sbuf("carry", [128, 48, 2], F32)
            ss = P.sbuf("ss", [128, 4], F32)
            ps = [P.psum(f"ps{k}", [128, 512], F32) for k in range(8)]

            cd = dict(sem="d_const", allow_slow_non_contiguous=True)
            P.dma("sp", g1T[:], dap(mix_norm_pre, 0, [[1, 128], [128, 8]]), writes=[g1T], sem="d_g1T", allow_slow_non_contiguous=True)
            P.op("pool", lambda e: e.memset(ident[:], 1.0), writes=[ident])
            P.op("pool", lambda e: e.affine_select(out=ident[:], in_=ident[:], pattern=[[1, 128]], compare_op=ALU.is_equal,
                                                   fill=0.0, base=0, channel_multiplier=-1), reads=[ident], writes=[ident])

            def const_dmas_early():
                P.dma("sp", bfb[:], dap(b_forget, 0, [[0, 128], [1, 16]]), writes=[bfb], sem="d_bfb", allow_slow_non_contiguous=True)

            def const_dmas_rg():
                grp = [caw, cab, bra, brx, lam, bmg]
                for kk in range(4):
                    P.dma("act", caw[:, :, kk], dap(conv_a_w, D_RNN * kk, [[1, 128], [128, 10]]), writes=[caw], **cd)
                for tbuf, src in ((cab, conv_a_b), (bra, b_rg_a), (brx, b_rg_x), (lam, lru_lambda)):
                    P.dma("act", tbuf[:], dap(src, 0, [[1, 128], [128, 10]]), writes=[tbuf], **cd)
                for kk in range(2):
                    P.dma("act", bmg[:, kk, :], dap(b_merge, D * kk, [[1, 128], [128, 8]]), writes=[bmg], **cd)
                tot = P.cnt["d_const"]
                for b in grp:
                    b.w = ("d_const", tot)

            def const_dmas_ffn():
                cd2 = dict(sem="d_const2", allow_slow_non_contiguous=True)
                P.dma("act", g3T[:], dap(ffn_norm_pre, 0, [[1, 128], [128, 8]]), writes=[g3T], **cd2)
                for kk in range(3):
                    P.dma("act", cfw[:, :, kk], dap(conv_f_w, 2 * D_FF * kk, [[1, 128], [128, 48]]), writes=[cfw], **cd2)
                P.dma("act", cfb[:], dap(conv_f_b, 0, [[1, 128], [128, 48]]), writes=[cfb], **cd2)
                tot2 = P.cnt["d_const2"]
                for b in (g3T, cfw, cfb):
                    b.w = ("d_const2", tot2)

            def late_setup():
                P.op("pool", lambda e: e.memset(Umat[:], 1.0), writes=[Umat])
                P.op("pool", lambda e: e.affine_select(out=Umat[:], in_=Umat[:], pattern=[[1, 128]], compare_op=ALU.is_ge,
                                                       fill=0.0, base=0, channel_multiplier=-1), reads=[Umat], writes=[Umat])
                P.op("pool", lambda e: e.memset(Ubc[:], 1.0), writes=[Ubc])
                P.op("pool", lambda e: e.affine_select(out=Ubc[:], in_=Ubc[:], pattern=[[0, 128]], compare_op=ALU.is_ge,
                                                       fill=0.0, base=63, channel_multiplier=-1), reads=[Ubc], writes=[Ubc])
                P.op("pool", lambda e: e.memset(ones[:], 1.0), writes=[ones])

            def rg_consts():
                P.op("act", lambda e: e.activation(out=lrc[:], in_=lam[:], func=AF.Exp, scale=-1.0), reads=[lam], writes=[lrc])
                P.op("act", lambda e: e.activation(out=lrc[:], in_=lrc[:], func=AF.Ln, bias=1.0), reads=[lrc], writes=[lrc])
                P.op("dve", lambda e: e.tensor_scalar(out=lrc2[:], in0=lrc[:], scalar1=-8.0, scalar2=None, op0=ALU.mult),
                     reads=[lrc], writes=[lrc2])
                P.op("dve", lambda e: e.tensor_scalar(out=lrc[:], in0=lrc[:], scalar1=-4.0, scalar2=None, op0=ALU.mult),
                     reads=[lrc], writes=[lrc])
                P.op("dve", lambda e: e.tensor_scalar(out=hbra[:], in0=bra[:], scalar1=0.5, scalar2=None, op0=ALU.mult),
                     reads=[bra], writes=[hbra])
                P.op("dve", lambda e: e.tensor_scalar(out=hbrx[:], in0=brx[:], scalar1=0.5, scalar2=None, op0=ALU.mult),
                     reads=[brx], writes=[hbrx])

            band_idx = {}
            for blk in range(16):
                r0, r1 = 80 * blk, 80 * blk + 80
                for ci in range(r0 // 128, (r1 - 1) // 128 + 1):
                    for co in range(r0 // 128, (r1 - 1) // 128 + 1):
                        if (ci, co) not in band_idx:
                            band_idx[(ci, co)] = len(band_idx)
            NBAND = len(band_idx)
            Wgd = P.mk(177, [NBAND, 2, 128], BF16, "Wgd")

            def wgd_items():
                items = []
                for g, wsrc in ((0, w_rg_a), (1, w_rg_x)):
                    for blk in range(16):
                        r0, r1 = 80 * blk, 80 * blk + 80
                        for ci in range(r0 // 128, (r1 - 1) // 128 + 1):
                            a0, a1 = max(r0, 128 * ci), min(r1, 128 * ci + 128)
                            for co in range(r0 // 128, (r1 - 1) // 128 + 1):
                                c0, c1 = max(r0, 128 * co), min(r1, 128 * co + 128)
                                items.append(lambda g=g, wsrc=wsrc, blk=blk, ci=ci, co=co, a0=a0, a1=a1, c0=c0, c1=c1, r0=r0: P.dma(
                                    "pool", Wgd[a0 - 128 * ci:a1 - 128 * ci, band_idx[(ci, co)], g, c0 - 128 * co:c1 - 128 * co],
                                    wsrc[blk, a0 - r0:a1 - r0, c0 - r0:c1 - r0], writes=[Wgd], sem="d_wgd"))
                return items

            evac_flip = [0]

            def evac_scaled(out_ap, in_ap, scal_ap, reads, writes):
                evac_flip[0] ^= 1
                if evac_flip[0]:
                    P.op("act", lambda e: e.activation(out=out_ap, in_=in_ap, func=AF.Copy, scale=scal_ap), reads=reads, writes=writes)
                else:
                    P.op("dve", lambda e: e.tensor_scalar(out=out_ap, in0=in_ap, scalar1=scal_ap, scalar2=None, op0=ALU.mult),
                         reads=reads, writes=writes)

            def evac_copy(out_ap, in_ap, reads, writes):
                evac_flip[0] ^= 1
                if evac_flip[0]:
                    P.op("act", lambda e: e.activation(out=out_ap, in_=in_ap, func=AF.Copy), reads=reads, writes=writes)
                else:
                    P.op("dve", lambda e: e.tensor_copy(out=out_ap, in_=in_ap), reads=reads, writes=writes)

            def rms_rstd(src_ap, src_bufs, junk, ssb, rstd):
                P.op("act", lambda e: e.activation(out=junk[:], in_=src_ap, func=AF.Square, accum_out=ssb[:, 0:1]),
                     reads=src_bufs, writes=[junk, ssb])
                P.op("act", lambda e: e.activation(out=ssb[:, 1:2], in_=ssb[:, 0:1], func=AF.Sqrt, scale=1.0 / D, bias=EPS),
                     reads=[ssb], writes=[ssb])
                P.op("dve", lambda e: e.reciprocal(out=rstd[:], in_=ssb[:, 1:2]), reads=[ssb], writes=[rstd])

            const_dmas_early()
            hT = P.mk(0, [8, T], BF16, "hT", nsub=4)
            hTa = hT[0].t
            xt = [P.mk(152 + 4 * k, [D], F32, f"xt{k}") for k in range(2)] + [P.mk(36 + 4 * k, [D], F32, f"xt{2 + k}") for k in range(4)]
            xn4 = P.mk(160, [4, D], F32, "xn4")
            junk = P.mk(32, [D], BF16, "junk")
            rstd = P.sbuf("rstd", [128, 1], F32)
            ss4 = [P.sbuf(f"ss4_{k}", [128, 4], F32) for k in range(4)]
            rstd4 = [P.sbuf(f"rstd4_{k}", [128, 1], F32) for k in range(4)]

            def norm_part(get_tile):
                for j in range(4):
                    src_ap, src_bufs = get_tile(j)
                    rms_rstd(src_ap, src_bufs, junk, ss4[j], rstd4[j])
                    P.op("dve", lambda e, j=j, src_ap=src_ap: e.tensor_scalar(out=xn4[:, j, :], in0=src_ap, scalar1=rstd4[j][:, 0:1],
                                                                            scalar2=None, op0=ALU.mult),
                         reads=src_bufs + [rstd4[j]], writes=[xn4])

            def transpose_part(gT, dst_ap, dst_buf, col0, tp_banks):
                for c in range(8):
                    tp = tp_banks[c % 2]
                    for j in range(4):
                        P.op("pe", lambda e, j=j, c=c, tp=tp: e.transpose(out=tp[:, j * 128:(j + 1) * 128],
                                                                         in_=xn4[:, j, c * 128:(c + 1) * 128], identity=ident[:]),
                             reads=[xn4, ident], writes=[tp], sig=(j == 3))
                    evac_scaled(dst_ap[:, c, col0:col0 + 512], tp[:], gT[:, c:c + 1], [tp, gT], [dst_buf])

            def norm_transpose(get_tile, gT, dst_ap, dst_buf, col0, tp_banks):
                norm_part(get_tile)
                transpose_part(gT, dst_ap, dst_buf, col0, tp_banks)

            for tt in range(4):
                def get_tile(j, tt=tt):
                    i = 4 * tt + j
                    b = xt[i % 6]
                    P.dma("sp", b[:], x[i * 128:(i + 1) * 128, :], writes=[b], sem=f"d_xt{i % 6}")
                    return b[:], [b]
                norm_transpose(get_tile, g1T, hTa, hT[tt], tt * 512, ps[0:2])

            late_setup()
            const_dmas_rg()
            if debug:
                P.dma("sp", dbg["hT"], hTa, reads=hT, sem="dbg")
            y_b = P.mk(32, [8, T], BF16, "y_b", nsub=4)
            yba = y_b[0].t
            Btab = P.mk(64, [16, 8, 16], F32, "Btab")
            wqkv = [P.mk(80 + 6 * k, [8, 3, 128], BF16, f"wqkv{k}") for k in range(3)]
            qz = [P.mk(98 + 8 * k, [2, T], BF16, f"qz{k}") for k in range(2)]
            kp = [P.mk(114 + 4 * k, [T], BF16, f"kp{k}") for k in range(2)]
            Vp = [P.mk(122 + 8 * k, [16, 2, 128], BF16, f"Vp{k}") for k in range(2)]
            Pt = [P.mk(138 + k, [512], BF16, f"Pt{k}") for k in range(4)]
            rc = [P.mk(142 + 2 * k, [512], F32, f"rc{k}") for k in range(2)]
            wf = P.mk(146, [8, 16], BF16, "wf")
            zf = P.mk(147, [16, 16], F32, "zf")
            Lf = P.mk(148, [16, 16], F32, "Lf")
            cumL = P.mk(149, [16, 16], F32, "cumL")
            Cmid = P.mk(150, [16, 16], F32, "Cmid")
            Pfx = P.mk(151, [16, 16], F32, "Pfx")

            P.dma("pool", wf[:], w_in[:, O_F:O_F + 16].rearrange("(c p) n -> p c n", p=128), writes=[wf], sem="d_wf")
            for i in range(NT):
                for c in range(8):
                    P.op("pe", lambda e, i=i, c=c: e.matmul(ps[2][:, i * 16:(i + 1) * 16], lhsT=hTa[:, c, i * 128:(i + 1) * 128],
                                                           rhs=wf[:, c, :], start=(c == 0), stop=(c == 7)),
                         reads=[hT[i // 4], wf], writes=[ps[2]], sig=(c == 7))
            P.op("dve", lambda e: e.tensor_tensor(out=zf[:], in0=ps[2][:, 0:256].rearrange("p (a b) -> p a b", a=16),
                                                  in1=bfb[:].unsqueeze(1).broadcast_to([128, 16, 16]), op=ALU.add),
                 reads=[ps[2], bfb], writes=[zf])
            P.op("act", lambda e: e.activation(out=Lf[:], in_=zf[:], func=AF.Exp, scale=-1.0), reads=[zf], writes=[Lf])
            P.op("act", lambda e: e.activation(out=Lf[:], in_=Lf[:], func=AF.Ln, bias=1.0), reads=[Lf], writes=[Lf])
            Lflat = Lf[:].rearrange("p a b -> p (a b)")
            P.op("pe", lambda e: e.matmul(ps[3][:, 0:256], lhsT=Umat[:], rhs=Lflat, start=True, stop=True), reads=[Umat, Lf], writes=[ps[3]])
            P.op("pe", lambda e: e.matmul(ps[3][:, 256:512], lhsT=Ubc[:], rhs=Lflat, start=True, stop=True), reads=[Ubc, Lf], writes=[ps[3]])
            P.op("pe", lambda e: e.matmul(ps[4][:, 0:256], lhsT=ones[:], rhs=Lflat, start=True, stop=True), reads=[ones, Lf], writes=[ps[4]])
            P.op("dve", lambda e: e.memset(Pfx[:], 0.0), writes=[Pfx])
            for i in range(1, NT):
                P.op("dve", lambda e, i=i: e.tensor_tensor(out=Pfx[:, i, :], in0=Pfx[:, i - 1, :], in1=ps[4][:, (i - 1) * 16:i * 16], op=ALU.add),
                     reads=[Pfx, ps[4]], writes=[Pfx])
            P.op("dve", lambda e: e.tensor_tensor(out=cumL[:], in0=ps[3][:, 0:256].rearrange("p (a b) -> p a b", a=16), in1=Pfx[:], op=ALU.add),
                 reads=[ps[3], Pfx], writes=[cumL])
            P.op("dve", lambda e: e.tensor_tensor(out=Cmid[:], in0=ps[3][:, 256:512].rearrange("p (a b) -> p a b", a=16), in1=Pfx[:], op=ALU.add),
                 reads=[ps[3], Pfx], writes=[Cmid])
            for g in range(8):
                nj_ = 2 * g + 2
                P.op("dve", lambda e: e.tensor_tensor(out=Btab[:, :, g, 0:nj_], in0=cumL[:, 0:nj_, :].rearrange("p j h -> p h j"),
                                                      in1=Pfx[:, 2 * g + 1, :].unsqueeze(2).broadcast_to([128, 16, nj_]), op=ALU.subtract),
                     reads=[cumL, Pfx], writes=[Btab])

            def att_memset(k):
                P.op("pool", lambda e: e.memset(qz[k][:].bitcast(F32), 0.0), writes=[qz[k]])
                P.op("pool", lambda e: e.memset(Vp[k][:].bitcast(F32), ONES_BF16X2), writes=[Vp[k]])

            att_memset(0)

            S_banks = [ps[2], ps[3], ps[4]]
            O_banks = [ps[5], ps[6]]
            pj_banks = [ps[0], ps[1], ps[7]]
            cnt_s = [0]
            cnt_o = [0]
            cnt_pj = [0]
            cnt_pt = [0]
            cnt_rc = [0]
            PtH = [[P.alias(Pt[k], f"Pt{k}h{hf}") for hf in range(2)] for k in range(4)]

            def att_load(p):
                w = wqkv[p % 3]
                for wi, off in enumerate((O_Q, O_K, O_V)):
                    P.dma("pool", w[:, :, wi, :], w_in[:, off + 128 * p: off + 128 * (p + 1)].rearrange("(c p) n -> p c n", p=128),
                          writes=[w], sem=f"d_wqkv{p % 3}")

            def att_proj(p):
                w = wqkv[p % 3]
                qzb, kpb, Vpb = qz[p % 2], kp[p % 2], Vp[p % 2]
                items = []

                def qk_item(tt, wi):
                    bank = pj_banks[cnt_pj[0] % 3]
                    cnt_pj[0] += 1
                    for c in range(8):
                        P.op("pe", lambda e: e.matmul(bank[:], lhsT=w[:, c, wi, :], rhs=hTa[:, c, tt * 512:(tt + 1) * 512],
                                                      start=(c == 0), stop=(c == 7)),
                             reads=[w, hT[tt]], writes=[bank], sig=(c == 7))
                    if wi == 0:
                        P.op("dve", lambda e: e.tensor_copy(out=qzb[0:64, 0, tt * 512:(tt + 1) * 512], in_=bank[0:64, :]), reads=[bank], writes=[qzb])
                        P.op("dve", lambda e: e.tensor_copy(out=qzb[64:128, 1, tt * 512:(tt + 1) * 512], in_=bank[64:128, :]), reads=[bank], writes=[qzb])
                    else:
                        P.op("dve", lambda e: e.tensor_copy(out=kpb[:, tt * 512:(tt + 1) * 512], in_=bank[:]), reads=[bank], writes=[kpb])

                def v_item(g):
                    bank = pj_banks[cnt_pj[0] % 3]
                    cnt_pj[0] += 1
                    for jj in range(4):
                        i = 4 * g + jj
                        for c in range(8):
                            P.op("pe", lambda e: e.matmul(bank[:, jj * 128:(jj + 1) * 128], lhsT=hTa[:, c, i * 128:(i + 1) * 128],
                                                          rhs=w[:, c, 2, :], start=(c == 0), stop=(c == 7)),
                                 reads=[w, hT[g]], writes=[bank], sig=(c == 7 and jj == 3))
                    bv = bank[:].rearrange("p (a b) -> p a b", a=4)
                    P.op("dve", lambda e: e.tensor_copy(out=Vpb[:, 4 * g:4 * g + 4, 0, 0:64], in_=bv[:, :, 0:64]), reads=[bank], writes=[Vpb])
                    P.op("dve", lambda e: e.tensor_copy(out=Vpb[:, 4 * g:4 * g + 4, 1, 64:128], in_=bv[:, :, 64:128]), reads=[bank], writes=[Vpb])

                for tt in range(4):
                    for wi in range(2):
                        items.append(lambda tt=tt, wi=wi: qk_item(tt, wi))
                for g in range(4):
                    items.append(lambda g=g: v_item(g))
                return items

            pend = []
            att_load(0)
            att_load(1)
            P.op("pool", lambda e: e.memset(Wgd[:].bitcast(F32), 0.0), writes=[Wgd])
            for it_ in wgd_items():
                it_()
            wgd_q = []
            wrx = [P.mk(164.75 + 2 * k, [8, 128], BF16, f"wrx{k}") for k in range(3)]
            wrg = [P.mk(170.75 + 2 * k, [8, 128], BF16, f"wrg{k}") for k in range(3)]
            NU = 40
            def rg_load_x(u):
                co = u % 10
                w = wrx[u % 3]
                P.dma("pool", w[:], w_in[:, O_XA + 128 * co:O_XA + 128 * (co + 1)].rearrange("(c p) n -> p c n", p=128), writes=[w], sem=f"d_wrx{u % 3}")

            def rg_load_g(u):
                co = u % 10
                w = wrg[u % 3]
                P.dma("pool", w[:], w_in[:, O_GA + 128 * co:O_GA + 128 * (co + 1)].rearrange("(c p) n -> p c n", p=128), writes=[w], sem=f"d_wrg{u % 3}")

            for u in range(3):
                rg_load_x(u)
            for u in range(3):
                rg_load_g(u)
            for it_ in att_proj(0):
                it_()
            for p in range(8):
                if p + 2 < 8:
                    att_load(p + 2)
                nxt = att_proj(p + 1) if p + 1 < 8 else []
                if p == 4:
                    rg_consts()
                if p == 0:
                    att_memset(1)
                    P.op("pool", lambda e: e.memset(carry[:], 0.0), writes=[carry])
                step = [0]
                qzb, kpb, Vpb = qz[p % 2], kp[p % 2], Vp[p % 2]
                for hh in range(2):
                    h = 2 * p + hh
                    orow = slice(0, 64) if hh == 0 else slice(64, 128)
                    drow = slice(64, 128) if hh == 0 else slice(0, 64)
                    for c in range(4):
                        O = O_banks[cnt_o[0] % 2]
                        cnt_o[0] += 1
                        nj = 4 * c + 4

                        def qk(j):
                            S = S_banks[cnt_s[0] % 3]
                            cnt_s[0] += 1
                            q0 = max(128 * j, 512 * c)
                            rel = q0 - 512 * c
                            P.op("pe", lambda e: e.matmul(S[:, rel:512], lhsT=kpb[:, 128 * j:128 * (j + 1)], rhs=qzb[:, hh, q0:512 * (c + 1)],
                                                          start=True, stop=True), reads=[kpb, qzb], writes=[S])
                            kpt = cnt_pt[0] % 4
                            cnt_pt[0] += 1
                            Pb = Pt[kpt]
                            used = []
                            for hf in range(2):
                                g = 2 * c + hf
                                lo = max(256 * hf, rel)
                                hi = 256 * (hf + 1)
                                if lo >= hi:
                                    continue
                                PH = PtH[kpt][hf]
                                used.append(PH)
                                P.op("act", lambda e: e.activation(out=Pb[:, lo:hi], in_=S[:, lo:hi], func=AF.Exp,
                                                                   scale=0.125, bias=Btab[:, h, g, j:j + 1]),
                                     reads=[S, Btab], writes=[PH])
                                dlo = 128 * j - 512 * c
                                if lo <= dlo < hi and j >= 4 * c:
                                    P.op("pool", lambda e: e.affine_select(out=Pb[:, dlo:dlo + 128], in_=Pb[:, dlo:dlo + 128], pattern=[[1, 128]],
                                                                           compare_op=ALU.is_ge, fill=0.0, base=0, channel_multiplier=-1),
                                         reads=[PH], writes=[PH])
                            return (j, rel, Pb, used)

                        def make_pv(item, O=O, hh=hh, nj=nj, Vpb=Vpb, p=p, c=c, orow=orow, drow=drow):
                            j, rel, Pb, used = item

                            def emit():
                                P.op("pe", lambda e: e.matmul(O[:, rel:512], lhsT=Vpb[:, j, hh, :], rhs=Pb[:, rel:512], start=(j == 0), stop=(j == nj - 1)),
                                     reads=[Vpb] + used, writes=[O], sig=(j == nj - 1))
                                if j == nj - 1:
                                    rcb = rc[cnt_rc[0] % 2]
                                    cnt_rc[0] += 1
                                    P.op("dve", lambda e: e.reciprocal(out=rcb[orow, :], in_=O[drow, :]), reads=[O], writes=[rcb])
                                    P.op("dve", lambda e: e.tensor_tensor(out=yba[orow, p, c * 512:(c + 1) * 512], in0=O[orow, :],
                                                                          in1=rcb[orow, :], op=ALU.mult),
                                         reads=[O, rcb], writes=[y_b[c]])
                            return emit

                        for j in range(nj):
                            pend.append(make_pv(qk(j)))
                            if len(pend) > 2:
                                pend.pop(0)()
                            step[0] += 1
                            if nxt and step[0] % 6 == 0:
                                nxt.pop(0)()
                while nxt:
                    nxt.pop(0)()
                for _ in range(12):
                    if wgd_q and p >= 1:
                        wgd_q.pop(0)()
            while pend:
                pend.pop(0)()
            while wgd_q:
                wgd_q.pop(0)()

            if debug:
                P.dma("sp", dbg["y_b"], yba, reads=y_b, sem="dbg")
                P.dma("sp", dbg["Btab"], Btab[:], reads=[Btab], sem="dbg")
            y_a = P.mk(64, [10, T], BF16, "y_a", nsub=4)
            yaa = y_a[0].t
            o = 128
            xa_sb = [P.mk(o + 2.25 * k, [576], F32, f"xa_sb{k}") for k in range(3)]
            o += 6.75
            xc = [P.mk(o + 2 * k, [512], F32, f"xc{k}") for k in range(5)] + [P.mk(120 + 2 * k, [512], F32, f"xc{5 + k}") for k in range(2)]
            o += 10
            xcb = [P.mk(o + k, [512], BF16, f"xcb{k}") for k in range(4)]
            o += 4
            rb = [P.mk(o + 2 * k, [512], F32, f"rb{k}") for k in range(2)] + [P.mk(104 + 2 * k, [512], F32, f"rb{2 + k}") for k in range(2)]
            o += 4
            igb = [P.mk(o + 2 * k, [512], F32, f"igb{k}") for k in range(2)] + [P.mk(108 + 2 * k, [512], F32, f"igb{2 + k}") for k in range(2)]
            o += 4
            a2b = [P.mk(o + 2 * k, [512], F32, f"a2b{k}") for k in range(2)] + [P.mk(112 + 2 * k, [512], F32, f"a2b{2 + k}") for k in range(2)]
            o += 4
            ggb = [P.mk(o + 2 * k, [512], F32, f"ggb{k}") for k in range(2)] + [P.mk(116 + 2 * k, [512], F32, f"ggb{2 + k}") for k in range(2)]
            o += 4
            xh = P.mk(190.5, [10, 3], F32, "xh")
            hcar = P.mk(190.75, [10], F32, "hcar")
            P.op("dve", lambda e: e.memset(xh[:], 0.0), writes=[xh])
            P.op("dve", lambda e: e.memset(hcar[:], 0.0), writes=[hcar])
            NU = 40

            def rg_A1(u):
                tt, co = divmod(u, 10)
                w = wrx[u % 3]
                xab, xcc, xcbb = xa_sb[u % 3], xc[u % 7], xcb[u % 4]
                tsl = slice(tt * 512, (tt + 1) * 512)
                pa_ = ps[u % 2]
                for c in range(8):
                    P.op("pe", lambda e: e.matmul(pa_[:], lhsT=w[:, c, :], rhs=hTa[:, c, tsl], start=(c == 0), stop=(c == 7)),
                         reads=[w, hT[tt]], writes=[pa_], sig=(c == 7))
                if u + 3 < NU:
                    rg_load_x(u + 3)
                P.op("act", lambda e: e.activation(out=xab[:, 0:3], in_=xh[:, co, :], func=AF.Copy), reads=[xh], writes=[xab])
                P.op("act", lambda e: e.activation(out=xab[:, 3:515], in_=pa_[:], func=AF.Copy), reads=[pa_], writes=[xab])
                P.op("act", lambda e: e.activation(out=xh[:, co, :], in_=pa_[:, 509:512], func=AF.Copy), reads=[pa_], writes=[xh])
                P.op("dve", lambda e: e.tensor_scalar(out=xcc[:], in0=xab[:, 0:512], scalar1=caw[:, co, 0:1], scalar2=cab[:, co:co + 1],
                                                      op0=ALU.mult, op1=ALU.add), reads=[xab, caw, cab], writes=[xcc])
                for kk in range(1, 4):
                    P.op("dve", lambda e: e.scalar_tensor_tensor(out=xcc[:], in0=xab[:, kk:kk + 512], scalar=caw[:, co, kk:kk + 1],
                                                                 in1=xcc[:], op0=ALU.mult, op1=ALU.add),
                         reads=[xab, caw, xcc], writes=[xcc])
                P.op("dve", lambda e: e.tensor_copy(out=xcbb[:], in_=xcc[:]), reads=[xcc], writes=[xcbb])

            def rg_A2(u):
                tt, co = divmod(u, 10)
                w = wrg[u % 3]
                tsl = slice(tt * 512, (tt + 1) * 512)
                s3 = 2 + 3 * (u % 2)
                pg_, pr_, pi_ = ps[s3], ps[s3 + 1], ps[s3 + 2]
                nb = [ci for ci in (co - 1, co, co + 1) if (ci, co) in band_idx]
                for g, pb_ in ((0, pr_), (1, pi_)):
                    for n_, ci in enumerate(nb):
                        xsrc = xcb[(u + ci - co) % 4]
                        P.op("pe", lambda e: e.matmul(pb_[:], lhsT=Wgd[:, band_idx[(ci, co)], g, :], rhs=xsrc[:], start=(n_ == 0), stop=(n_ == len(nb) - 1)),
                             reads=[Wgd, xsrc], writes=[pb_], sig=(n_ == len(nb) - 1))
                for c in range(8):
                    P.op("pe", lambda e: e.matmul(pg_[:], lhsT=w[:, c, :], rhs=hTa[:, c, tsl], start=(c == 0), stop=(c == 7)),
                         reads=[w, hT[tt]], writes=[pg_], sig=(c == 7))
                if u + 3 < NU:
                    rg_load_g(u + 3)

            def rg_bufs(u):
                k4 = u % 4
                s3 = 2 + 3 * (u % 2)
                return xc[u % 7], rb[k4], igb[k4], a2b[k4], ggb[k4], ps[s3], ps[s3 + 1], ps[s3 + 2]

            def rg_BG(u):
                co = u % 10
                xcc, rbb, igbb, a2bb, gg, pg_, pr_, pi_ = rg_bufs(u)
                P.op("act", lambda e: e.activation(out=gg[:], in_=pg_[:], func=AF.Gelu_apprx_tanh), reads=[pg_], writes=[gg])
                P.op("act", lambda e: e.activation(out=rbb[:], in_=pr_[:], func=AF.Tanh, scale=0.5, bias=hbra[:, co:co + 1]), reads=[pr_, hbra], writes=[rbb])
                P.op("act", lambda e: e.activation(out=igbb[:], in_=pi_[:], func=AF.Tanh, scale=0.5, bias=hbrx[:, co:co + 1]), reads=[pi_, hbrx], writes=[igbb])

            def rg_BE(u):
                co = u % 10
                xcc, rbb, igbb, a2bb, gg, pg_, pr_, pi_ = rg_bufs(u)
                P.op("act", lambda e: e.activation(out=a2bb[:], in_=rbb[:], func=AF.Exp, scale=lrc2[:, co:co + 1], bias=lrc2[:, co:co + 1]), reads=[rbb, lrc2], writes=[a2bb])
                P.op("act", lambda e: e.activation(out=rbb[:], in_=rbb[:], func=AF.Exp, scale=lrc[:, co:co + 1], bias=lrc[:, co:co + 1]), reads=[rbb, lrc], writes=[rbb])
                P.op("act", lambda e: e.activation(out=a2bb[:], in_=a2bb[:], func=AF.Ln, scale=-1.0, bias=1.0), reads=[a2bb], writes=[a2bb])
                P.op("act", lambda e: e.activation(out=a2bb[:], in_=a2bb[:], func=AF.Exp, scale=0.5), reads=[a2bb], writes=[a2bb])

            def rg_BD(u):
                tt, co = divmod(u, 10)
                xcc, rbb, igbb, a2bb, gg, pg_, pr_, pi_ = rg_bufs(u)
                tsl = slice(tt * 512, (tt + 1) * 512)
                P.op("dve", lambda e: e.scalar_tensor_tensor(out=igbb[:], in0=igbb[:], scalar=1.0, in1=xcc[:], op0=ALU.add, op1=ALU.mult),
                     reads=[igbb, xcc], writes=[igbb])
                P.op("dve", lambda e: e.scalar_tensor_tensor(out=igbb[:], in0=igbb[:], scalar=0.5, in1=a2bb[:], op0=ALU.mult, op1=ALU.mult),
                     reads=[igbb, a2bb], writes=[igbb])
                P.op("dve", lambda e: e.tensor_tensor_scan(out=a2bb[:], data0=rbb[:], data1=igbb[:], initial=hcar[:, co:co + 1],
                                                           op0=ALU.mult, op1=ALU.add), reads=[rbb, igbb, hcar], writes=[a2bb])
                P.op("dve", lambda e: e.tensor_copy(out=hcar[:, co:co + 1], in_=a2bb[:, 511:512]), reads=[a2bb], writes=[hcar])
                P.op("dve", lambda e: e.tensor_tensor(out=yaa[:, co, tsl], in0=gg[:], in1=a2bb[:], op=ALU.mult),
                     reads=[gg, a2bb], writes=[y_a[tt]])

            rg_A1(0)
            rg_A1(1)
            rg_A1(2)
            rg_A2(0)
            rg_A2(1)
            for u in range(0, NU, 2):
                for v in (u + 3, u + 4):
                    if v < NU:
                        rg_A1(v)
                rg_BG(u)
                rg_BG(u + 1)
                for v in (u + 2, u + 3):
                    if v < NU:
                        rg_A2(v)
                rg_BE(u)
                rg_BE(u + 1)
                if u >= 2:
                    rg_BD(u - 2)
                    rg_BD(u - 1)
            rg_BD(NU - 2)
            rg_BD(NU - 1)

            if debug:
                P.dma("sp", dbg["y_a"], yaa, reads=y_a, sem="dbg")
            merged = P.mk(128, [8, T], BF16, "merged", nsub=4)
            mga = merged[0].t
            wb_g = [P.mk(o_, [8, 2, 128], BF16, f"wb_g{k}") for k, o_ in enumerate((165, 177))]
            wb_a = [P.mk(o_, [10, 128], BF16, f"wb_a{k}") for k, o_ in enumerate((169, 181))]
            wb_b = [P.mk(o_, [8, 128], BF16, f"wb_b{k}") for k, o_ in enumerate((171.5, 183.5))]
            sga = [P.mk(o_, [512], F32, f"sga{k}") for k, o_ in enumerate((160, 187.5))]
            sgb = [P.mk(o_, [512], F32, f"sgb{k}") for k, o_ in enumerate((162, 189.5))]
            mab = [P.mk(o_, [512], F32, f"mab{k}") for k, o_ in enumerate((185.5, 124))]

            def b_load(co):
                k2 = co % 2
                csl = slice(128 * co, 128 * (co + 1))
                P.dma("pool", wb_g[k2][:, :, 0, :], w_in[:, O_GMA + 128 * co:O_GMA + 128 * (co + 1)].rearrange("(c p) n -> p c n", p=128), writes=[wb_g[k2]], sem=f"d_wbg{k2}")
                P.dma("pool", wb_g[k2][:, :, 1, :], w_in[:, O_GMB + 128 * co:O_GMB + 128 * (co + 1)].rearrange("(c p) n -> p c n", p=128), writes=[wb_g[k2]], sem=f"d_wbg{k2}")
                P.dma("pool", wb_a[k2][:], w_proj_a[:, csl].rearrange("(c p) n -> p c n", p=128), writes=[wb_a[k2]], sem=f"d_wba{k2}")
                P.dma("pool", wb_b[k2][:], w_proj_b[:, csl].rearrange("(c p) n -> p c n", p=128), writes=[wb_b[k2]], sem=f"d_wbb{k2}")

            b_load(0)
            wout = P.mk(104, [8, D], BF16, "wout")
            g2b = P.mk(120, [D], F32, "g2b")
            P.dma("sp", g2b[:], dap(mix_norm_post, 0, [[0, 128], [1, D]]), writes=[g2b], sem="d_g2b")
            it = 0
            for co in range(8):
                k2 = co % 2
                wg_, wa_, wb_ = wb_g[k2], wb_a[k2], wb_b[k2]
                for tt in range(4):
                    if tt == 1:
                        if co + 1 < 8:
                            b_load(co + 1)
                        if co == 4:
                            const_dmas_ffn()
                        if co == 2:
                            for c2 in range(2):
                                P.dma("pool", wout[:, 4 * c2:4 * c2 + 4, :], w_out[512 * c2:512 * (c2 + 1), :].rearrange("(c p) n -> p c n", p=128), writes=[wout], sem="d_wout")
                    s2 = it % 2
                    it += 1
                    tsl = slice(tt * 512, (tt + 1) * 512)
                    pga, pua, pgb, pub = ps[0 + 4 * s2], ps[1 + 4 * s2], ps[2 + 4 * s2], ps[3 + 4 * s2]
                    sa, sb, ma = sga[s2], sgb[s2], mab[s2]
                    for c in range(8):
                        P.op("pe", lambda e: e.matmul(pga[:], lhsT=wg_[:, c, 0, :], rhs=hTa[:, c, tsl], start=(c == 0), stop=(c == 7)),
                             reads=[wg_, hT[tt]], writes=[pga], sig=(c == 7))
                    for c in range(8):
                        P.op("pe", lambda e: e.matmul(pgb[:], lhsT=wg_[:, c, 1, :], rhs=hTa[:, c, tsl], start=(c == 0), stop=(c == 7)),
                             reads=[wg_, hT[tt]], writes=[pgb], sig=(c == 7))
                    for bb in range(10):
                        P.op("pe", lambda e: e.matmul(pua[:], lhsT=wa_[:, bb, :], rhs=yaa[:, bb, tsl], start=(bb == 0), stop=(bb == 9)),
                             reads=[wa_, y_a[tt]], writes=[pua], sig=(bb == 9))
                    for c in range(8):
                        P.op("pe", lambda e: e.matmul(pub[:], lhsT=wb_[:, c, :], rhs=yba[:, c, tsl], start=(c == 0), stop=(c == 7)),
                             reads=[wb_, y_b[tt]], writes=[pub], sig=(c == 7))
                    P.op("act", lambda e: e.activation(out=sa[:], in_=pga[:], func=AF.Sigmoid, bias=bmg[:, 0, co:co + 1]), reads=[pga, bmg], writes=[sa])
                    P.op("act", lambda e: e.activation(out=sb[:], in_=pgb[:], func=AF.Sigmoid, bias=bmg[:, 1, co:co + 1]), reads=[pgb, bmg], writes=[sb])
                    P.op("dve", lambda e: e.tensor_tensor(out=ma[:], in0=pua[:], in1=sa[:], op=ALU.mult), reads=[pua, sa], writes=[ma])
                    P.op("dve", lambda e: e.tensor_tensor(out=sb[:], in0=pub[:], in1=sb[:], op=ALU.mult), reads=[pub, sb], writes=[sb])
                    P.op("dve", lambda e: e.tensor_tensor(out=mga[:, co, tsl], in0=ma[:], in1=sb[:], op=ALU.add),
                         reads=[ma, sb], writes=[merged[tt]])
            if debug:
                P.dma("sp", dbg["merged"], mga, reads=merged, sem="dbg")
            wu = [P.mk(64 + 4 * k, [8, 2, 128], BF16, f"wu{k}") for k in range(4)]
            def d_load(g):
                n = g % 24
                w = wu[g % 4]
                P.dma("pool", w[:, :, 0, :], w_up[:, 128 * n:128 * (n + 1)].rearrange("(c p) n -> p c n", p=128), writes=[w], sem=f"d_wu{g % 4}")
                P.dma("pool", w[:, :, 1, :], w_up[:, D_FF + 128 * n:D_FF + 128 * (n + 1)].rearrange("(c p) n -> p c n", p=128), writes=[w], sem=f"d_wu{g % 4}")

            x1 = P.mk(0, [NT, D], F32, "x1", nsub=NT)
            x1a = x1[0].t
            g4b = P.mk(84, [D], F32, "g4b")
            xt = [P.mk(88 + 4 * k, [D], F32, f"xtc{k}") for k in range(2)]
            tmp = P.mk(96, [D], F32, "tmp")
            junk = P.mk(100, [D], BF16, "junkc")
            ssc = P.sbuf("ssc", [128, 4], F32)
            rstd2 = P.sbuf("rstd2", [128, 1], F32)
            P.dma("sp", g4b[:], dap(ffn_norm_post, 0, [[0, 128], [1, D]]), writes=[g4b], sem="d_g4b")

            def post_norm_residual(pbanks, res_ap, res_bufs, gb, out_ap, out_bufs):
                for hf in range(2):
                    P.op("act", lambda e, hf=hf: e.activation(out=junk[:, hf * 512:(hf + 1) * 512], in_=pbanks[hf][:], func=AF.Square, accum_out=ssc[:, hf:hf + 1]),
                         reads=[pbanks[hf]], writes=[junk, ssc])
                P.op("dve", lambda e: e.tensor_tensor(out=ssc[:, 2:3], in0=ssc[:, 0:1], in1=ssc[:, 1:2], op=ALU.add), reads=[ssc], writes=[ssc])
                P.op("act", lambda e: e.activation(out=ssc[:, 3:4], in_=ssc[:, 2:3], func=AF.Sqrt, scale=1.0 / D, bias=EPS), reads=[ssc], writes=[ssc])
                P.op("dve", lambda e: e.reciprocal(out=rstd2[:], in_=ssc[:, 3:4]), reads=[ssc], writes=[rstd2])
                for hf in range(2):
                    hs_ = slice(hf * 512, (hf + 1) * 512)
                    P.op("dve", lambda e, hf=hf, hs_=hs_: e.scalar_tensor_tensor(out=tmp[:, hs_], in0=pbanks[hf][:], scalar=rstd2[:, 0:1], in1=gb[:, hs_],
                                                                              op0=ALU.mult, op1=ALU.mult), reads=[pbanks[hf], rstd2, gb], writes=[tmp])
                P.op("dve", lambda e: e.tensor_tensor(out=out_ap, in0=tmp[:], in1=res_ap, op=ALU.add), reads=[tmp] + res_bufs, writes=out_bufs)

            for i in range(NT):
                bk = [ps[2 * (i % 4)], ps[2 * (i % 4) + 1]]
                for hf in range(2):
                    for c in range(8):
                        P.op("pe", lambda e, c=c, i=i, hf=hf, bk=bk: e.matmul(bk[hf][:], lhsT=mga[:, c, i * 128:(i + 1) * 128], rhs=wout[:, c, hf * 512:(hf + 1) * 512],
                                                                           start=(c == 0), stop=(c == 7)), reads=[merged[i // 4], wout], writes=[bk[hf]], sig=(c == 7))
                xb_ = xt[i % 2]
                P.dma("sp", xb_[:], x[i * 128:(i + 1) * 128, :], writes=[xb_], sem=f"d_xtc{i % 2}")
                post_norm_residual(bk, xb_[:], [xb_], g2b, x1a[:, i, :], [x1[i]])
                if i == 5:
                    for g in range(3):
                        d_load(g)

            if debug:
                P.dma("sp", dbg["x1"], x1a, reads=x1, sem="dbg")
            wd = P.mk(112, [24, D], BF16, "wd")
            act_s = P.mk(160, [24, 512], BF16, "act", nsub=6)
            act = act_s[0].t
            h2T = P.mk(184, [8, 512], BF16, "h2T")
            tmp = P.mk(80, [D], F32, "tmpd")
            junk2 = P.mk(109, [D], BF16, "junkd2")

            def ffn_get_tile(tc):
                def get_tile(j):
                    i = 4 * tc + j
                    return x1a[:, i, :], [x1[i]]
                return get_tile

            xn4 = P.mk(88, [4, D], F32, "xn4d")
            junk = P.mk(109, [D], BF16, "junkd")
            norm_transpose(ffn_get_tile(0), g3T, h2T.t, h2T, 0, ps[0:2])
            for c6 in range(6):
                P.dma("pool", wd[:, 4 * c6:4 * c6 + 4, :], w_down[512 * c6:512 * (c6 + 1), :].rearrange("(c p) n -> p c n", p=128), writes=[wd], sem="d_wd")
            for tc in range(4):
                ug = [P.mk(88 + 2.25 * k, [576], F32, f"ug{k}") for k in range(2)]
                uv = [P.mk(92.5 + 2.25 * k, [576], F32, f"uv{k}") for k in range(2)]
                cvg = [P.mk(97 + 2 * k, [512], F32, f"cvg{k}") for k in range(3)]
                cvv = [P.mk(103 + 2 * k, [512], F32, f"cvv{k}") for k in range(3)]

                def d_A(n):
                    g = 24 * tc + n
                    w = wu[g % 4]
                    if g + 3 < 96:
                        d_load(g + 3)
                    k2 = n % 2
                    for wi, (ub, cv) in enumerate(((ug[k2], cvg[n % 3]), (uv[k2], cvv[n % 3]))):
                        ch = n + 24 * wi
                        bank = ps[2 + (2 * n + wi) % 4]
                        for c in range(8):
                            P.op("pe", lambda e: e.matmul(bank[:], lhsT=w[:, c, wi, :], rhs=h2T[:, c, :], start=(c == 0), stop=(c == 7)),
                                 reads=[w, h2T], writes=[bank], sig=(c == 7))
                        P.op("act", lambda e: e.activation(out=ub[:, 0:2], in_=carry[:, ch, :], func=AF.Copy), reads=[carry], writes=[ub])
                        P.op("act", lambda e: e.activation(out=ub[:, 2:514], in_=bank[:], func=AF.Copy), reads=[bank], writes=[ub])
                        P.op("act", lambda e: e.activation(out=cv[:], in_=bank[:], func=AF.Identity, scale=cfw[:, ch, 2:3], bias=cfb[:, ch:ch + 1]),
                             reads=[bank, cfw, cfb], writes=[cv])
                        P.op("act", lambda e: e.activation(out=carry[:, ch, :], in_=bank[:, 510:512], func=AF.Copy), reads=[bank], writes=[carry])

                def d_B(n):
                    k2 = n % 2
                    for wi, (ub, cv) in enumerate(((ug[k2], cvg[n % 3]), (uv[k2], cvv[n % 3]))):
                        ch = n + 24 * wi
                        for kk in range(2):
                            P.op("dve", lambda e: e.scalar_tensor_tensor(out=cv[:], in0=ub[:, kk:kk + 512], scalar=cfw[:, ch, kk:kk + 1],
                                                                         in1=cv[:], op0=ALU.mult, op1=ALU.add),
                                 reads=[ub, cfw, cv], writes=[cv])
                        if wi == 0:
                            P.op("act", lambda e: e.activation(out=cv[:], in_=cv[:], func=AF.Gelu_apprx_tanh), reads=[cv], writes=[cv])

                def d_M(n):
                    P.op("dve", lambda e: e.tensor_tensor(out=act[:, n, :], in0=cvg[n % 3][:], in1=cvv[n % 3][:], op=ALU.mult),
                         reads=[cvg[n % 3], cvv[n % 3]], writes=[act_s[n // 4]])

                d_A(0)
                for n in range(24):
                    if n + 1 < 24:
                        d_A(n + 1)
                    d_B(n)
                    if n >= 1:
                        d_M(n - 1)
                d_M(23)
                if tc + 1 < 4:
                    xn4 = P.mk(88, [4, D], F32, "xn4d")
                    junk = P.mk(109, [D], BF16, "junkd")
                    norm_part(ffn_get_tile(tc + 1))
                junk_save = junk
                junk = junk2
                for j in range(4):
                    i = 4 * tc + j
                    bk = [ps[2 + 2 * (j % 2)], ps[3 + 2 * (j % 2)]]
                    for hf in range(2):
                        for kc in range(24):
                            P.op("pe", lambda e: e.matmul(bk[hf][:], lhsT=act[:, kc, j * 128:(j + 1) * 128], rhs=wd[:, kc, hf * 512:(hf + 1) * 512],
                                                          start=(kc == 0), stop=(kc == 23)), reads=[act_s[kc // 4], wd], writes=[bk[hf]], sig=(kc == 23))
                    post_norm_residual(bk, x1a[:, i, :], [x1[i]], g4b, x1a[:, i, :], [x1[i]])
                    P.dma("sp", y[i * 128:(i + 1) * 128, :], x1a[:, i, :], reads=[x1[i]], sem=f"st{j % 2}")
                junk = junk_save
                if tc + 1 < 4:
                    transpose_part(g3T, h2T.t, h2T, 0, ps[0:2])
            P.wait_all("sp", ["st0", "st1", "dbg"])
        P.run(program)
    return nc


_NC_CACHE = {}


def kernel(**inputs):
    n = 8
    if "nc" not in _NC_CACHE:
        _NC_CACHE["nc"] = build()
    nc = _NC_CACHE["nc"]
    f = lambda a: np.ascontiguousarray(np.asarray(a, dtype=np.float32))
    xs = f(inputs["x"])
    shared = {
        "mix_norm_pre": f(inputs["mix_norm_pre"]).reshape(D),
        "mix_norm_post": f(inputs["mix_norm_post"]).reshape(D),
        "w_in": f(inputs["w_in"]).reshape(D, D_IN),
        "conv_a_w": f(inputs["conv_a_w"]).reshape(4, D_RNN),
        "conv_a_b": f(inputs["conv_a_b"]).reshape(D_RNN),
        "w_rg_a": f(inputs["w_rg_a"]).reshape(16, 80, 80),
        "b_rg_a": f(inputs["b_rg_a"]).reshape(D_RNN),
        "w_rg_x": f(inputs["w_rg_x"]).reshape(16, 80, 80),
        "b_rg_x": f(inputs["b_rg_x"]).reshape(D_RNN),
        "lru_lambda": f(inputs["lru_lambda"]).reshape(D_RNN),
        "b_forget": f(inputs["b_forget"]).reshape(16),
        "b_merge": f(inputs["b_merge"]).reshape(2, D),
        "w_proj_a": f(inputs["w_proj_a"]).reshape(D_RNN, D),
        "w_proj_b": f(inputs["w_proj_b"]).reshape(D_ATT, D),
        "w_out": f(inputs["w_out"]).reshape(D, D),
        "ffn_norm_pre": f(inputs["ffn_norm_pre"]).reshape(D),
        "ffn_norm_post": f(inputs["ffn_norm_post"]).reshape(D),
        "w_up": f(inputs["w_up"]).reshape(D, 2 * D_FF),
        "conv_f_w": f(inputs["conv_f_w"]).reshape(3, 2 * D_FF),
        "conv_f_b": f(inputs["conv_f_b"]).reshape(2 * D_FF),
        "w_down": f(inputs["w_down"]).reshape(D_FF, D),
    }
    in_maps = [dict(shared, x=xs[b]) for b in range(n)]
    res = run_bass_kernel_spmd(nc, in_maps, core_ids=list(range(n)))
    return np.stack([np.asarray(r["y"], dtype=np.float32) for r in res.results], axis=0)
```
